# Optimizing a Trainium2 kernel written in Bass

```python
import math
import jax, jax.numpy as jnp
from jax import lax
import numpy as np

D_MODEL = 2048
BATCH = 2
SEQ = 4096
DEPTH = 1

N_META = 16
Q_BLOCK = 128
MLA_HEADS = 8
MLA_NOPE = 128
MLA_ROPE = 64
MLA_V = 128
Q_LORA = 512
KV_LORA = 256
ROPE_THETA = 10000.0
DIFF_HEADS = 4
DIFF_DK = 128
DIFF_V = 2 * DIFF_DK
DIFF_QK = DIFF_HEADS * 2 * DIFF_DK
D_MIX = MLA_HEADS * MLA_V + DIFF_HEADS * DIFF_V
NUM_BUCKETS = 32
REL_MAX_DIST = 128
N_BIAS_MAPS = 2 * DIFF_HEADS
D_FF = 5632
CONV_W = 3
EPS = 1e-6
IN_SPLITS = (Q_LORA, KV_LORA, MLA_ROPE, DIFF_QK, DIFF_QK, DIFF_HEADS * DIFF_V)
D_IN = sum(IN_SPLITS)

kernel_name = "hymba_mla_diffattn_convffn_layer"


def rmsnorm(x, g):
    xf = x.astype(jnp.float32)
    y = xf * lax.rsqrt(jnp.mean(xf * xf, axis=-1, keepdims=True) + EPS)
    return (y * g.astype(jnp.float32)).astype(x.dtype)


def rope(x, pos):
    half = x.shape[-1] // 2
    inv = ROPE_THETA ** (-jnp.arange(half, dtype=jnp.float32) / half)
    ang = pos.astype(jnp.float32)[:, None] * inv[None, :]
    shp = (1, pos.shape[0]) + (1,) * (x.ndim - 3) + (half,)
    cos, sin = jnp.cos(ang).reshape(shp), jnp.sin(ang).reshape(shp)
    xf = x.astype(jnp.float32)
    x1, x2 = xf[..., :half], xf[..., half:]
    return jnp.concatenate([x1 * cos - x2 * sin, x2 * cos + x1 * sin], axis=-1).astype(x.dtype)


def t5_bucket(rel):
    n = jnp.maximum(rel, 0)
    max_exact = NUM_BUCKETS // 2
    nf = jnp.maximum(n, 1).astype(jnp.float32)
    large = max_exact + (jnp.log(nf / max_exact) / math.log(REL_MAX_DIST / max_exact)
                         * (NUM_BUCKETS - max_exact)).astype(jnp.int32)
    large = jnp.minimum(large, NUM_BUCKETS - 1)
    return jnp.where(n < max_exact, n, large)


def token_mixer(n, pos, rel_bias, w_in, w_uq, w_ukv, w_o, g_cq, g_ckv,
                lq1, lk1, lq2, lk2, g_sub, layer_idx):
    B, Lp, _ = n.shape
    nb = Lp // Q_BLOCK
    proj = n @ w_in
    cuts = list(np.cumsum(IN_SPLITS)[:-1])
    c_q, c_kv, k_r, q_d, k_d, v_d = jnp.split(proj, cuts, axis=-1)

    c_q = rmsnorm(c_q, g_cq)
    c_kv = rmsnorm(c_kv, g_ckv)
    q_m = (c_q @ w_uq).reshape(B, Lp, MLA_HEADS, MLA_NOPE + MLA_ROPE)
    q_nope = q_m[..., :MLA_NOPE]
    q_rope = rope(q_m[..., MLA_NOPE:], pos)
    kv = (c_kv @ w_ukv).reshape(B, Lp, MLA_HEADS, MLA_NOPE + MLA_V)
    k_nope = kv[..., :MLA_NOPE].transpose(0, 2, 1, 3)
    v_m = kv[..., MLA_NOPE:].transpose(0, 2, 1, 3)
    k_rope = rope(k_r, pos)
    mla_scale = 1.0 / math.sqrt(MLA_NOPE + MLA_ROPE)

    q_d = q_d.reshape(B, Lp, DIFF_HEADS, 2, DIFF_DK)
    k_d = k_d.reshape(B, Lp, DIFF_HEADS, 2, DIFF_DK).transpose(0, 2, 3, 1, 4)
    v_d = v_d.reshape(B, Lp, DIFF_HEADS, DIFF_V).transpose(0, 2, 1, 3)
    lam_init = 0.8 - 0.6 * math.exp(-0.3 * layer_idx)
    lam = (jnp.exp(jnp.sum(lq1.astype(jnp.float32) * lk1.astype(jnp.float32)))
           - jnp.exp(jnp.sum(lq2.astype(jnp.float32) * lk2.astype(jnp.float32))) + lam_init)
    diff_scale = 1.0 / math.sqrt(DIFF_DK)

    qn_b = q_nope.reshape(B, nb, Q_BLOCK, MLA_HEADS, MLA_NOPE).transpose(1, 0, 3, 2, 4)
    qr_b = q_rope.reshape(B, nb, Q_BLOCK, MLA_HEADS, MLA_ROPE).transpose(1, 0, 3, 2, 4)
    qd_b = q_d.reshape(B, nb, Q_BLOCK, DIFF_HEADS, 2, DIFF_DK).transpose(1, 0, 3, 4, 2, 5)
    q_start = jnp.arange(nb, dtype=jnp.int32) * Q_BLOCK
    kpos = jnp.arange(Lp, dtype=jnp.int32)
    rel_bias_f = rel_bias.astype(jnp.float32)

    def attend(args):
        qn, qr, qd, q0 = args
        qpos = q0 + jnp.arange(Q_BLOCK, dtype=jnp.int32)
        rel = qpos[:, None] - kpos[None, :]
        causal = rel >= 0
        s_m = (jnp.einsum('bhqd,bhkd->bhqk', qn, k_nope)
               + jnp.einsum('bhqr,bkr->bhqk', qr, k_rope)).astype(jnp.float32) * mla_scale
        p_m = jax.nn.softmax(jnp.where(causal, s_m, -jnp.inf), axis=-1)
        o_m = jnp.einsum('bhqk,bhkd->bqhd', p_m.astype(v_m.dtype), v_m)
        bias = rel_bias_f[t5_bucket(rel)]
        bias = bias.reshape(Q_BLOCK, Lp, DIFF_HEADS, 2).transpose(2, 3, 0, 1)
        s_d = jnp.einsum('bhcqd,bhckd->bhcqk', qd, k_d).astype(jnp.float32) * diff_scale + bias
        p_d = jax.nn.softmax(jnp.where(causal, s_d, -jnp.inf), axis=-1)
        a = p_d[:, :, 0] - lam * p_d[:, :, 1]
        o_d = jnp.einsum('bhqk,bhkd->bqhd', a.astype(v_d.dtype), v_d)
        return o_m, o_d

    o_m, o_d = lax.map(attend, (qn_b, qr_b, qd_b, q_start))
    o_m = o_m.transpose(1, 0, 2, 3, 4).reshape(B, Lp, MLA_HEADS * MLA_V)
    o_d = o_d.transpose(1, 0, 2, 3, 4).reshape(B, Lp, DIFF_HEADS, DIFF_V)
    o_d = (rmsnorm(o_d, g_sub) * (1.0 - lam_init)).reshape(B, Lp, DIFF_HEADS * DIFF_V)
    return jnp.concatenate([o_m, o_d], axis=-1) @ w_o


def conv_gated_mlp(n, w_up, conv_w, conv_b, w_down):
    Lp = n.shape[1]
    up = n @ w_up
    upp = jnp.pad(up, ((0, 0), (CONV_W - 1, 0), (0, 0)))
    conv = conv_b
    for j in range(CONV_W):
        conv = conv + conv_w[j] * upp[:, j:j + Lp]
    gate, val = jnp.split(conv, 2, axis=-1)
    return (jax.nn.silu(gate) * val) @ w_down


def setup_inputs(seed: int = 0) -> dict:
    key = jax.random.key(seed)
    ks = jax.random.split(key, 24)
    f32 = jnp.float32
    nrm = lambda k, shp, s: jax.random.normal(k, shp, f32) * s
    gain = lambda k, shp: 1.0 + 0.05 * jax.random.normal(k, shp, f32)
    L = DEPTH
    return {
        "x": nrm(ks[0], (BATCH, SEQ, D_MODEL), 1.0),
        "meta_tokens": nrm(ks[1], (N_META, D_MODEL), 1.0),
        "rel_bias": nrm(ks[2], (NUM_BUCKETS, N_BIAS_MAPS), 0.5),
        "w_in": nrm(ks[3], (L, D_MODEL, D_IN), D_MODEL ** -0.5),
        "w_uq": nrm(ks[4], (L, Q_LORA, MLA_HEADS * (MLA_NOPE + MLA_ROPE)), Q_LORA ** -0.5),
        "w_ukv": nrm(ks[5], (L, KV_LORA, MLA_HEADS * (MLA_NOPE + MLA_V)), KV_LORA ** -0.5),
        "w_o": nrm(ks[6], (L, D_MIX, D_MODEL), D_MIX ** -0.5),
        "g_attn_pre": gain(ks[7], (L, D_MODEL)),
        "g_attn_post": gain(ks[8], (L, D_MODEL)),
        "g_cq": gain(ks[9], (L, Q_LORA)),
        "g_ckv": gain(ks[10], (L, KV_LORA)),
        "lambda_q1": nrm(ks[11], (L, DIFF_DK), 0.1),
        "lambda_k1": nrm(ks[12], (L, DIFF_DK), 0.1),
        "lambda_q2": nrm(ks[13], (L, DIFF_DK), 0.1),
        "lambda_k2": nrm(ks[14], (L, DIFF_DK), 0.1),
        "g_diff_sub": gain(ks[15], (L, DIFF_V)),
        "g_ffn_pre": gain(ks[16], (L, D_MODEL)),
        "g_ffn_post": gain(ks[17], (L, D_MODEL)),
        "w_up": nrm(ks[18], (L, D_MODEL, 2 * D_FF), D_MODEL ** -0.5),
        "conv_w": nrm(ks[19], (L, CONV_W, 2 * D_FF), CONV_W ** -0.5),
        "conv_b": nrm(ks[20], (L, 2 * D_FF), 0.01),
        "w_down": nrm(ks[21], (L, D_FF, D_MODEL), D_FF ** -0.5),
    }


def reference(x, meta_tokens, rel_bias, w_in, w_uq, w_ukv, w_o, g_attn_pre, g_attn_post,
              g_cq, g_ckv, lambda_q1, lambda_k1, lambda_q2, lambda_k2, g_diff_sub,
              g_ffn_pre, g_ffn_post, w_up, conv_w, conv_b, w_down):
    B, S, D = x.shape
    L = S + N_META
    Lp = ((L + Q_BLOCK - 1) // Q_BLOCK) * Q_BLOCK
    meta = jnp.broadcast_to(meta_tokens[None].astype(x.dtype), (B, N_META, D))
    pad = jnp.zeros((B, Lp - L, D), x.dtype)
    h = jnp.concatenate([meta, x, pad], axis=1)
    pos = jnp.arange(Lp, dtype=jnp.int32)
    for l in range(DEPTH):
        n = rmsnorm(h, g_attn_pre[l])
        a = token_mixer(n, pos, rel_bias, w_in[l], w_uq[l], w_ukv[l], w_o[l], g_cq[l], g_ckv[l],
                        lambda_q1[l], lambda_k1[l], lambda_q2[l], lambda_k2[l], g_diff_sub[l], l)
        h = h + rmsnorm(a, g_attn_post[l])
        n = rmsnorm(h, g_ffn_pre[l])
        f = conv_gated_mlp(n, w_up[l], conv_w[l], conv_b[l], w_down[l])
        h = h + rmsnorm(f, g_ffn_post[l])
    return h[:, N_META:N_META + S]
```

```python
import math
from contextlib import ExitStack

import numpy as np
import concourse.bass as bass
import concourse.mybir as mybir
from concourse.bass_utils import run_bass_kernel_spmd

F32 = mybir.dt.float32
BF16 = mybir.dt.bfloat16
ALU = mybir.AluOpType
AF = mybir.ActivationFunctionType

NCORES = 8
TKV = 4112
NQ = 1040
QB = 130
EPS = 1e-6
NEG = -30000.0
CG = [(0, 390), (390, 390), (780, 260)]
GB = [(0, 3), (3, 6), (6, 8)]


class _Op:
    __slots__ = ("stream", "eng", "fn", "deps", "signal", "count")


class Sched:
    ENGS = ("pe", "act", "dve", "pool", "sp")

    NDMASEM = 16

    def __init__(self, same_engine_sync=True):
        self.q = {e: [] for e in self.ENGS}
        self.lastw = {}
        self.readers = {}
        self.last_real = {}
        self.same_engine_sync = same_engine_sync
        self.dma_rr = {}

    def add(self, eng, fn, reads=(), writes=(), dma=False):
        op = _Op()
        op.eng = eng
        if dma:
            k = self.dma_rr.get(eng, 0)
            self.dma_rr[eng] = k + 1
            op.stream = "dma_%s_%d" % (eng, k % self.NDMASEM)
        else:
            op.stream = eng
        op.fn = fn
        op.signal = bool(dma)
        op.count = 0
        deps = []
        seen = set()

        def _add(d):
            if d is None or id(d) in seen:
                return
            seen.add(id(d))
            if d.stream == op.stream and not dma:
                if op.stream == "pe" or not self.same_engine_sync:
                    return
            deps.append(d)

        if dma:
            _add(self.last_real.get(op.stream))
        for k in reads:
            _add(self.lastw.get(k))
        for k in writes:
            _add(self.lastw.get(k))
            for r in self.readers.get(k, {}).values():
                _add(r)
        op.deps = deps
        for d in deps:
            d.signal = True
        for k in reads:
            self.readers.setdefault(k, {})[op.stream] = op
        for k in writes:
            self.lastw[k] = op
            self.readers[k] = {}
        self.q[eng].append(op)
        self.last_real[op.stream] = op
        return op

    def barrier(self):
        lasts = list(self.last_real.values())
        for e in self.ENGS:
            op = _Op()
            op.eng = e
            op.stream = e
            op.fn = None
            op.signal = False
            op.count = 0
            op.deps = list(lasts)
            for d in lasts:
                d.signal = True
            self.q[e].append(op)
        self.lastw = {}
        self.readers = {}

    def finalize(self):
        cnt = {}
        for e in self.ENGS:
            for op in self.q[e]:
                if op.signal:
                    cnt[op.stream] = cnt.get(op.stream, 0) + 1
                    op.count = cnt[op.stream]
        for s, c in cnt.items():
            assert c * (16 if s.startswith("dma_") else 1) < 60000, (s, c)
        self.totals = cnt
        self.streams = sorted(set(list(cnt.keys()) + list(self.ENGS)))

    def emit(self, eng_name, engine, sems):
        waited = {}
        for op in self.q[eng_name]:
            for d in op.deps:
                val = d.count * (16 if d.stream.startswith("dma_") else 1)
                if waited.get(d.stream, 0) < val:
                    engine.wait_ge(sems[d.stream], val)
                    waited[d.stream] = val
            if op.fn is None:
                continue
            ins = op.fn(engine)
            if op.signal:
                ins.then_inc(sems[op.stream], 16 if op.stream.startswith("dma_") else 1)


def build_nc():
    nc = bass.Bass("TRN2", target_bir_lowering=False)
    S = Sched()

    def din(name, shape, dt=F32):
        return nc.dram_tensor(name, shape, dt, kind="ExternalInput")

    xkv = din("xkv", [128, 16, TKV])
    xq = din("xq", [128, 16, NQ])
    wkv = din("wkv", [128, 16, 2432])
    wukv = din("wukv", [128, 2, 2048])
    wq = din("wq", [128, 16, 1536])
    wuq = din("wuq", [128, 4, 2048])
    wo = din("wo", [16, 128, 2048])
    wup = din("wup", [44, 128, 4096])
    wdn = din("wdn", [16, 128, 5632])
    gains = din("gains", [128, 72])
    convp = din("convp", [128, 352])
    rb = din("rb", [32, 8])
    rb31 = din("rb31", [128, 8])
    lamv = din("lamv", [128, 4])
    oh = din("oh", [33, 769])
    cosq = din("cosq", [64, NQ])
    sinq = din("sinq", [64, NQ])
    cosk = din("cosk", [64, TKV])
    sink = din("sink", [64, TKV])
    out = nc.dram_tensor("out", [128, 16, 1024], F32, kind="ExternalOutput")

    kn_s = nc.dram_tensor("kn_s", [8, 128, TKV], BF16)
    kd_s = nc.dram_tensor("kd_s", [8, 128, TKV], BF16)
    kr_s = nc.dram_tensor("kr_s", [64, TKV], BF16)
    vm_s = nc.dram_tensor("vm_s", [TKV, 1024], BF16)
    vd_s = nc.dram_tensor("vd_s", [TKV, 1024], BF16)
    tscr = nc.dram_tensor("tscr", [9, 128, 769], F32)
    h1_s = nc.dram_tensor("h1_s", [128, 16, 1024], F32)

    def mm(o, lhsT, rhs, start, stop, reads, writes):
        S.add("pe", lambda e: e.matmul(o, lhsT=lhsT, rhs=rhs, start=start, stop=stop), reads, writes)

    def act(o, i, func, reads, writes, bias=0.0, scale=1.0):
        S.add("act", lambda e: e.activation(out=o, in_=i, func=func, bias=bias, scale=scale), reads, writes)

    def tt(eng, o, a, b, op, reads, writes):
        S.add(eng, lambda e: e.tensor_tensor(out=o, in0=a, in1=b, op=op), reads, writes)

    def ts(eng, o, a, s1, s2, op0, op1, reads, writes):
        if op1 is None:
            S.add(eng, lambda e: e.tensor_scalar(out=o, in0=a, scalar1=s1, scalar2=None, op0=op0), reads, writes)
        else:
            S.add(eng, lambda e: e.tensor_scalar(out=o, in0=a, scalar1=s1, scalar2=s2, op0=op0, op1=op1), reads, writes)

    def stt(eng, o, a, sc, b, op0, op1, reads, writes):
        S.add(eng, lambda e: e.scalar_tensor_tensor(out=o, in0=a, scalar=sc, in1=b, op0=op0, op1=op1), reads, writes)

    def recip(o, i, reads, writes):
        S.add("dve", lambda e: e.reciprocal(out=o, in_=i), reads, writes)

    def vcopy(o, i, reads, writes):
        S.add("dve", lambda e: e.tensor_copy(out=o, in_=i), reads, writes)

    def memset(eng, o, v, writes):
        S.add(eng, lambda e: e.memset(o, v), (), writes)

    def dma(eng, o, i, reads, writes):
        S.add(eng, lambda e: e.dma_start(out=o, in_=i), reads, writes, dma=True)

    evac_flip = [0]

    def evac(o, i, reads, writes):
        evac_flip[0] ^= 1
        if evac_flip[0]:
            act(o, i, AF.Copy, reads, writes)
        else:
            vcopy(o, i, reads, writes)

    with ExitStack() as top:
        def sbt(es, name, shape, dt, side=None):
            if side is None:
                return es.enter_context(nc.sbuf_tensor(name, shape, dt))
            return es.enter_context(nc.sbuf_tensor(name, shape, dt, side=side))

        ps = [top.enter_context(nc.psum_tensor("ps%d" % b, [128, 512], F32)) for b in range(8)]

        rot = {"n": 0}

        def next_bank(lo=0, hi=8):
            b = lo + rot["n"] % (hi - lo)
            rot["n"] += 1
            return b

        def PS(b):
            return ("ps", b)

        gains_s = sbt(top, "gains_s", [128, 72], F32)
        convp_s = sbt(top, "convp_s", [128, 352], F32)
        cm_s = sbt(top, "cm_s", [128, 8], F32)
        ones_bf = sbt(top, "ones_bf", [128, 128], BF16)
        ones_f = sbt(top, "ones_f", [128, 128], F32)
        eps_s = sbt(top, "eps_s", [128, 1], F32)
        lam_s = sbt(top, "lam_s", [128, 4], F32)
        gsub_s = sbt(top, "gsub_s", [128, 2], F32)
        right_ctx = ExitStack()
        btile = sbt(right_ctx, "btile", [128, 9 * 6, QB], F32, side="right")

        dma("sp", gains_s[:], gains.ap(), (), ["gains"])
        dma("sp", convp_s[:], convp.ap(), (), ["convp"])
        dma("sp", cm_s[:], rb31.ap(), (), ["cm"])
        memset("dve", ones_bf[:], 1.0, ["ones_bf"])
        memset("dve", ones_f[:], 1.0, ["ones_f"])
        memset("dve", eps_s[:], EPS, ["eps"])

        setup_ctx = ExitStack()
        lamv_s = sbt(setup_ctx, "lamv_s", [128, 4], F32, side="right")
        prod_s = sbt(setup_ctx, "prod_s", [128, 2], F32, side="right")
        e_s = sbt(setup_ctx, "e_s", [128, 2], F32, side="right")
        rb_s = sbt(setup_ctx, "rb_s", [32, 8], F32, side="right")
        oh_s = sbt(setup_ctx, "oh_s", [33, 769], F32, side="right")
        lhsb = sbt(setup_ctx, "lhsb", [33, 9, 128], F32, side="right")
        T_s = sbt(setup_ctx, "T_s", [128, 769], F32, side="right")

        def emit_setup():
            dma("sp", lamv_s[:], lamv.ap(), (), ["lamv"])
            dma("sp", rb_s[:], rb.ap(), (), ["rb"])
            dma("sp", oh_s[:], oh.ap(), (), ["oh"])
            tt("dve", prod_s[:], lamv_s[:, 0:2], lamv_s[:, 2:4], ALU.mult, ["lamv"], ["prod"])
            b0 = next_bank()
            mm(ps[b0][:, 0:2], ones_f[:, :], prod_s[:, :], True, True, ["ones_f", "prod"], [PS(b0)])
            act(e_s[:], ps[b0][:, 0:2], AF.Exp, [PS(b0)], ["e_s"])
            tt("dve", lam_s[:, 0:1], e_s[:, 0:1], e_s[:, 1:2], ALU.subtract, ["e_s"], ["lam"])
            ts("dve", lam_s[:, 0:1], lam_s[:, 0:1], 0.2, None, ALU.add, None, ["lam"], ["lam"])
            ts("dve", lam_s[:, 1:2], lam_s[:, 0:1], -1.0, None, ALU.mult, None, ["lam"], ["lam"])
            ts("dve", gsub_s[:], gains_s[:, 70:72], 0.8, None, ALU.mult, None, ["gains"], ["gsub"])
            tt("dve", rb_s[:], rb_s[:], cm_s[0:32, :], ALU.subtract, ["rb", "cm"], ["rb"])
            ts("dve", rb_s[:], rb_s[:], math.sqrt(128.0), None, ALU.mult, None, ["rb"], ["rb"])
            memset("dve", lhsb[:], 1.0, ["lhsb"])
            for m in range(8):
                ts("dve", lhsb[0:32, m, :], lhsb[0:32, m, :], rb_s[:, m:m + 1], None, ALU.mult, None,
                   ["lhsb", "rb"], ["lhsb"])
            memset("dve", lhsb[0:32, 8, :], 0.0, ["lhsb"])
            memset("dve", lhsb[32:33, :, :], NEG, ["lhsb"])

        def emit_setup_map(m):
            ba, bb = next_bank(), next_bank()
            mm(ps[ba][:, 0:512], lhsb[:, m, :], oh_s[:, 0:512], True, True, ["lhsb", "oh"], [PS(ba)])
            mm(ps[bb][:, 0:257], lhsb[:, m, :], oh_s[:, 512:769], True, True, ["lhsb", "oh"], [PS(bb)])
            vcopy(T_s[:, 0:512], ps[ba][:, 0:512], [PS(ba)], ["T_s"])
            vcopy(T_s[:, 512:769], ps[bb][:, 0:257], [PS(bb)], ["T_s"])
            dma("pool", tscr.ap()[m], T_s[:], ["T_s"], [("tscr", m)])
            for jx in range(5):
                jj = jx - 1
                src = bass.AP(tensor=tscr, offset=m * 128 * 769 + 511 - 128 * jj, ap=[[768, 128], [1, QB]])
                dma("pool", btile[:, m * 6 + jx, :], src, [("tscr", m)], ["btile"])
            src = bass.AP(tensor=tscr, offset=m * 128 * 769 + 527, ap=[[768, 16], [1, QB]])
            dma("pool", btile[0:16, m * 6 + 5, :], src, [("tscr", m)], ["btile"])

        def rstd_from(bank, n, nfeat, rs_ap, rs_key):
            act(rs_ap, ps[bank][:, 0:n], AF.Ln, [PS(bank), "eps"], [rs_key], bias=eps_s[:, 0:1], scale=1.0 / nfeat)
            act(rs_ap, rs_ap, AF.Exp, [rs_key], [rs_key], bias=0.0, scale=-0.5)

        with ExitStack() as es:
            wkv_s = sbt(es, "wkv_s", [128, 16, 2432], BF16)
            wukv_s = sbt(es, "wukv_s", [128, 2, 2048], BF16)
            xg = sbt(es, "xg", [128, 16, 256], F32)
            sq = sbt(es, "sq", [128, 16, 256], BF16)
            nT = [sbt(es, "nT%d" % i, [128, 16, 256], BF16) for i in range(2)]
            rs = sbt(es, "rs", [128, 256], F32)
            rs2 = sbt(es, "rs2", [128, 256], F32)
            ck = [sbt(es, "ck%d" % i, [64, 256], F32) for i in range(2)]
            sk = [sbt(es, "sk%d" % i, [64, 256], F32) for i in range(2)]
            r1 = sbt(es, "r1", [64, 256], F32)
            r2 = sbt(es, "r2", [64, 256], F32)
            kr_o = sbt(es, "kr_o", [64, 256], BF16)
            kd_o = sbt(es, "kd_o", [128, 8, 256], BF16)
            kn_o = sbt(es, "kn_o", [128, 8, 256], BF16)
            vd_o = sbt(es, "vd_o", [128, 2, 1024], BF16)
            vm_o = sbt(es, "vm_o", [128, 2, 1024], BF16)
            ckv_f = sbt(es, "ckv_f", [128, 2, 256], F32)
            ckv_sq = sbt(es, "ckv_sq", [128, 2, 256], BF16)
            ckvn = sbt(es, "ckvn", [128, 2, 256], BF16)

            WP = [(0, 384), (384, 896), (896, 1408), (1408, 1920), (1920, 2432)]
            for pi_, (a, b_) in enumerate(WP):
                dma("pool", wkv_s[:, :, a:b_], wkv.ap()[:, :, a:b_], ([("wkv", pi_ - 1)] if pi_ else ()),
                    [("wkv", pi_)])
            dma("pool", wukv_s[:], wukv.ap(), [("wkv", len(WP) - 1)], ["wukv"])

            def WK(col):
                for pi_, (a, b_) in enumerate(WP):
                    if a <= col < b_:
                        return ("wkv", pi_)

            groups = [(t0, 256) for t0 in range(0, 4096, 256)] + [(4096, 16)]

            def load_x(gi):
                t0, n = groups[gi]
                sl = gi % 2
                dma("sp", xg[:, :, 0:n], xkv.ap()[:, :, t0:t0 + n], (), ["xg"])
                dma("sp", ck[sl][:, 0:n], cosk.ap()[:, t0:t0 + n], (), [("ck", sl)])
                dma("sp", sk[sl][:, 0:n], sink.ap()[:, t0:t0 + n], (), [("sk", sl)])

            def pre_square(gi):
                t0, n = groups[gi]
                act(sq[:, :, 0:n], xg[:, :, 0:n], AF.Square, ["xg"], ["sq"])

            def prologue(gi):
                t0, n = groups[gi]
                sl = gi % 2
                nTc = nT[sl]
                b = next_bank()
                for kc in range(16):
                    mm(ps[b][:, 0:n], ones_bf[:, :], sq[:, kc, 0:n], kc == 0, kc == 15, ["ones_bf", "sq"], [PS(b)])
                rstd_from(b, n, 2048.0, rs[:, 0:n], "rs")
                for kc in range(16):
                    stt("dve", nTc[:, kc, 0:n], xg[:, kc, 0:n], gains_s[:, kc:kc + 1], rs[:, 0:n], ALU.mult, ALU.mult,
                        ["xg", "gains", "rs"], [("nT", sl)])

            def fm_chunk(gi, col0, ncol_out, pbank=None, pcol=0):
                t0, n = groups[gi]
                sl = gi % 2
                b = next_bank() if pbank is None else pbank
                for kc in range(16):
                    mm(ps[b][0:ncol_out, pcol:pcol + n], wkv_s[:, kc, col0:col0 + ncol_out], nT[sl][:, kc, 0:n],
                       kc == 0, kc == 15, [WK(col0), ("nT", sl)], [PS(b)])
                return b

            def body1(gi):
                t0, n = groups[gi]
                sl = gi % 2
                for c in range(2):
                    b = fm_chunk(gi, c * 128, 128)
                    act(ckv_f[:, c, 0:n], ps[b][:, 0:n], AF.Copy, [PS(b)], [("ckv_f", c)])
                    act(ckv_sq[:, c, 0:n], ps[b][:, 0:n], AF.Square, [PS(b)], [("ckv_sq", c)])
                b = fm_chunk(gi, 256, 64)
                fm_chunk(gi, 320, 64, pbank=b, pcol=256)
                tt("dve", r1[:, 0:n], ps[b][0:64, 0:n], ck[sl][:, 0:n], ALU.mult, [PS(b), ("ck", sl)], ["r1"])
                tt("dve", r2[:, 0:n], ps[b][0:64, 256:256 + n], sk[sl][:, 0:n], ALU.mult, [PS(b), ("sk", sl)], ["r2"])
                tt("dve", kr_o[:, 0:n], r1[:, 0:n], r2[:, 0:n], ALU.add, ["r1", "r2"], ["kr_o"])
                dma("sp", kr_s.ap()[:, t0:t0 + n], kr_o[:, 0:n], ["kr_o"], ["kr_s"])
                for m in range(4):
                    b = fm_chunk(gi, 384 + m * 128, 128)
                    evac(kd_o[:, m, 0:n], ps[b][:, 0:n], [PS(b)], ["kd_o"])
                b = next_bank()
                for c in range(2):
                    mm(ps[b][:, 0:n], ones_bf[:, :], ckv_sq[:, c, 0:n], c == 0, c == 1,
                       ["ones_bf", ("ckv_sq", c)], [PS(b)])
                rstd_from(b, n, 256.0, rs2[:, 0:n], "rs2")
                for c in range(2):
                    stt("dve", ckvn[:, c, 0:n], ckv_f[:, c, 0:n], gains_s[:, 68 + c:69 + c], rs2[:, 0:n],
                        ALU.mult, ALU.mult, [("ckv_f", c), "gains", "rs2"], ["ckvn"])

            def body2(gi):
                t0, n = groups[gi]
                sl = gi % 2
                nTc = nT[sl]
                for m in range(4, 8):
                    b = fm_chunk(gi, 384 + m * 128, 128)
                    evac(kd_o[:, m, 0:n], ps[b][:, 0:n], [PS(b)], ["kd_o"])
                dma("sp", kd_s.ap()[:, :, t0:t0 + n].rearrange("m p t -> p m t"), kd_o[:, :, 0:n], ["kd_o"], ["kd_s"])
                tbs = [(0, 128), (128, 128)] if n == 256 else [(0, n)]
                for ti, (o0, tn) in enumerate(tbs):
                    for half in range(2):
                        b = next_bank()
                        cc0 = 1408 + half * 512
                        for kc in range(16):
                            mm(ps[b][0:tn, 0:512], nTc[:, kc, o0:o0 + tn], wkv_s[:, kc, cc0:cc0 + 512],
                               kc == 0, kc == 15, [WK(cc0), ("nT", sl)], [PS(b)])
                        evac(vd_o[0:tn, ti, half * 512:(half + 1) * 512], ps[b][0:tn, 0:512], [PS(b)], [("vd_o", ti)])
                    dma("sp", vd_s.ap()[t0 + o0:t0 + o0 + tn, :], vd_o[0:tn, ti, :], [("vd_o", ti)], ["vd_s"])
                for h in range(8):
                    b = next_bank()
                    for c in range(2):
                        mm(ps[b][:, 0:n], wukv_s[:, c, h * 128:(h + 1) * 128], ckvn[:, c, 0:n], c == 0, c == 1,
                           ["wukv", "ckvn"], [PS(b)])
                    evac(kn_o[:, h, 0:n], ps[b][:, 0:n], [PS(b)], ["kn_o"])
                dma("sp", kn_s.ap()[:, :, t0:t0 + n].rearrange("m p t -> p m t"), kn_o[:, :, 0:n], ["kn_o"], ["kn_s"])
                for ti, (o0, tn) in enumerate(tbs):
                    for half in range(2):
                        b = next_bank()
                        for c in range(2):
                            mm(ps[b][0:tn, 0:512], ckvn[:, c, o0:o0 + tn],
                               wukv_s[:, c, 1024 + half * 512:1024 + (half + 1) * 512], c == 0, c == 1,
                               ["wukv", "ckvn"], [PS(b)])
                        evac(vm_o[0:tn, ti, half * 512:(half + 1) * 512], ps[b][0:tn, 0:512], [PS(b)], [("vm_o", ti)])
                    dma("sp", vm_s.ap()[t0 + o0:t0 + o0 + tn, :], vm_o[0:tn, ti, :], [("vm_o", ti)], ["vm_s"])

            emit_setup()
            load_x(0)
            pre_square(0)
            prologue(0)
            load_x(1)
            for gi in range(len(groups)):
                if 1 <= gi and gi + 1 < len(groups):
                    pre_square(gi + 1)
                body1(gi)
                if gi == 0:
                    pre_square(1)
                if gi + 1 < len(groups):
                    prologue(gi + 1)
                    if gi + 2 < len(groups):
                        load_x(gi + 2)
                if 1 <= gi <= 9:
                    emit_setup_map(gi - 1)
                body2(gi)
            S.barrier()
        setup_ctx.close()


        with ExitStack() as esq:
            qn = sbt(esq, "qn", [128, 8, NQ], BF16, side="right")
            qr = sbt(esq, "qr", [128, 8, NQ], BF16, side="right")
            qd = sbt(esq, "qd", [128, 8, NQ], BF16, side="right")

            with ExitStack() as es:
                wq_s = sbt(es, "wq_s", [128, 16, 1536], BF16)
                wuq_s = sbt(es, "wuq_s", [128, 4, 2048], BF16)
                xg = sbt(es, "xgq", [128, 16, 260], F32)
                sq = sbt(es, "sqq", [128, 16, 260], BF16)
                nTq2 = [sbt(es, "nTq%d" % i, [128, 16, 260], BF16) for i in range(2)]
                rs = sbt(es, "rsq", [128, 260], F32)
                rs2 = sbt(es, "rs2q", [128, 260], F32)
                cq_f = sbt(es, "cq_f", [128, 4, 260], F32)
                cq_sq = sbt(es, "cq_sq", [128, 4, 260], BF16)
                cqn = sbt(es, "cqn", [128, 4, 260], BF16)
                cq_s = [sbt(es, "cosq_s%d" % i, [64, 260], F32) for i in range(2)]
                sq_s = [sbt(es, "sinq_s%d" % i, [64, 260], F32) for i in range(2)]
                r1 = sbt(es, "r1q", [64, 260], F32)
                r2 = sbt(es, "r2q", [64, 260], F32)
                for q, (a_, b_) in enumerate([(0, 512), (512, 1024), (1024, 1536)]):
                    dma("pool", wq_s[:, :, a_:b_], wq.ap()[:, :, a_:b_], ([("wq", q - 1)] if q else ()), [("wq", q)])
                dma("pool", wuq_s[:], wuq.ap(), [("wq", 2)], ["wuq"])
                memset("pool", qr[64:128, :, :], 0.0, ["qr_pad"])
                WQ = []
                QG = [(0, 260), (260, 260), (520, 260), (780, 260)]

                def x_load(gi):
                    c0, n = QG[gi]
                    dma("sp", xg[:, :, 0:n], xq.ap()[:, :, c0:c0 + n], (), ["xg"])

                def cs_load(gi):
                    c0, n = QG[gi]
                    dma("sp", cq_s[gi % 2][:, 0:n], cosq.ap()[:, c0:c0 + n], (), [("cosq", gi % 2)])
                    dma("sp", sq_s[gi % 2][:, 0:n], sinq.ap()[:, c0:c0 + n], (), [("sinq", gi % 2)])

                def q_sq(gi):
                    c0, n = QG[gi]
                    act(sq[:, :, 0:n], xg[:, :, 0:n], AF.Square, ["xg"], ["sq"])

                def q_pro(gi):
                    c0, n = QG[gi]
                    b = next_bank()
                    for kc in range(16):
                        mm(ps[b][:, 0:n], ones_bf[:, :], sq[:, kc, 0:n], kc == 0, kc == 15, ["ones_bf", "sq"], [PS(b)])
                    rstd_from(b, n, 2048.0, rs[:, 0:n], "rs")
                    for kc in range(16):
                        stt("dve", nTq2[gi % 2][:, kc, 0:n], xg[:, kc, 0:n], gains_s[:, kc:kc + 1], rs[:, 0:n],
                            ALU.mult, ALU.mult, ["xg", "gains", "rs"], [("nTq", gi % 2)])

                def q_body_a(gi):
                    c0, n = QG[gi]
                    for c in range(4):
                        b = next_bank()
                        for kc in range(16):
                            mm(ps[b][:, 0:n], wq_s[:, kc, c * 128:(c + 1) * 128], nTq2[gi % 2][:, kc, 0:n],
                               kc == 0, kc == 15, [("wq", 0), ("nTq", gi % 2)], [PS(b)])
                        act(cq_f[:, c, 0:n], ps[b][:, 0:n], AF.Copy, [PS(b)], [("cq_f", c)])
                        act(cq_sq[:, c, 0:n], ps[b][:, 0:n], AF.Square, [PS(b)], [("cq_sq", c)])
                    b = next_bank()
                    for c in range(4):
                        mm(ps[b][:, 0:n], ones_bf[:, :], cq_sq[:, c, 0:n], c == 0, c == 3,
                           ["ones_bf", ("cq_sq", c)], [PS(b)])
                    rstd_from(b, n, 512.0, rs2[:, 0:n], "rs2")
                    for c in range(4):
                        stt("dve", cqn[:, c, 0:n], cq_f[:, c, 0:n], gains_s[:, 64 + c:65 + c], rs2[:, 0:n],
                            ALU.mult, ALU.mult, [("cq_f", c), "gains", "rs2"], ["cqn"])
                    for m in range(8):
                        b = next_bank()
                        for kc in range(16):
                            mm(ps[b][:, 0:n], wq_s[:, kc, 512 + m * 128:512 + (m + 1) * 128], nTq2[gi % 2][:, kc, 0:n],
                               kc == 0, kc == 15, [("wq", 1 + m // 4), ("nTq", gi % 2)], [PS(b)])
                        evac(qd[:, m, c0:c0 + n], ps[b][:, 0:n], [PS(b)], ["qd"])

                def q_body_b(gi):
                    c0, n = QG[gi]
                    for h in range(8):
                        b = next_bank()
                        for c in range(4):
                            mm(ps[b][:, 0:n], wuq_s[:, c, h * 256:h * 256 + 128], cqn[:, c, 0:n], c == 0, c == 3,
                               ["wuq", "cqn"], [PS(b)])
                        evac(qn[:, h, c0:c0 + n], ps[b][:, 0:n], [PS(b)], ["qn"])
                    for h in range(8):
                        ba, bb = next_bank(), next_bank()
                        for c in range(4):
                            mm(ps[ba][0:64, 0:n], wuq_s[:, c, h * 256 + 128:h * 256 + 192], cqn[:, c, 0:n],
                               c == 0, c == 3, ["wuq", "cqn"], [PS(ba)])
                        for c in range(4):
                            mm(ps[bb][0:64, 0:n], wuq_s[:, c, h * 256 + 192:h * 256 + 256], cqn[:, c, 0:n],
                               c == 0, c == 3, ["wuq", "cqn"], [PS(bb)])
                        tt("dve", r1[:, 0:n], ps[ba][0:64, 0:n], cq_s[gi % 2][:, 0:n], ALU.mult,
                           [PS(ba), ("cosq", gi % 2)], ["r1"])
                        tt("dve", r2[:, 0:n], ps[bb][0:64, 0:n], sq_s[gi % 2][:, 0:n], ALU.mult,
                           [PS(bb), ("sinq", gi % 2)], ["r2"])
                        tt("dve", qr[0:64, h, c0:c0 + n], r1[:, 0:n], r2[:, 0:n], ALU.add, ["r1", "r2"], ["qr"])

                x_load(0)
                cs_load(0)
                q_sq(0)
                q_pro(0)
                x_load(1)
                cs_load(1)
                for gi in range(4):
                    if gi + 1 < 4:
                        q_sq(gi + 1)
                        q_pro(gi + 1)
                        if gi + 2 < 4:
                            x_load(gi + 2)
                    q_body_a(gi)
                    q_body_b(gi)
                    if gi + 2 < 4:
                        cs_load(gi + 2)
                S.barrier()

            oT_ctx = ExitStack()
            oT = sbt(oT_ctx, "oT", [128, 16, NQ], BF16)

            with ExitStack() as es:
                kbuf = [sbt(es, "kbuf%d" % i, [128, 2, TKV], BF16) for i in range(2)]
                vbuf = [sbt(es, "vbuf%d" % i, [128, 33, 256], BF16) for i in range(2)]
                krT = sbt(es, "krT", [128, TKV], BF16)
                pT = [sbt(es, "pT%d" % i, [128, 390], BF16) for i in range(5)]
                rec = sbt(es, "rec", [128, 390], F32)
                A_s = sbt(es, "A_s", [128, 2, 390], F32)
                tmpd = sbt(es, "tmpd", [128, 390], F32)
                Oc = sbt(es, "Oc", [128, 2, 390], F32)
                sqd = sbt(es, "sqd", [128, 2, 390], BF16)
                rsd = sbt(es, "rsd", [128, 390], F32)
                dma("sp", krT[0:64, :], kr_s.ap(), ["kr_s"], ["krT"])
                memset("pool", krT[64:128, :], 0.0, ["krT_pad"])
                mla_scale = 1.0 / math.sqrt(192.0)
                diff_scale = 1.0 / math.sqrt(128.0)

                def load_unit(u):
                    sl = u % 2
                    if u < 8:
                        dma("sp", kbuf[sl][:, 0, :], kn_s.ap()[u], ["kn_s"], [("kbuf", sl)])
                        dma("sp", vbuf[sl][:, 0:32, 0:128],
                            vm_s.ap()[0:4096, u * 128:(u + 1) * 128].rearrange("(b p) f -> p b f", p=128),
                            ["vm_s"], [("vbuf", sl)])
                        dma("sp", vbuf[sl][0:16, 32, 0:128], vm_s.ap()[4096:4112, u * 128:(u + 1) * 128],
                            ["vm_s"], [("vbuf", sl)])
                    else:
                        hd = u - 8
                        dma("sp", kbuf[sl][:, :, :], kd_s.ap()[2 * hd:2 * hd + 2].rearrange("m p t -> p m t"),
                            ["kd_s"], [("kbuf", sl)])
                        dma("sp", vbuf[sl][:, 0:32, :],
                            vd_s.ap()[0:4096, hd * 256:(hd + 1) * 256].rearrange("(b p) f -> p b f", p=128),
                            ["vd_s"], [("vbuf", sl)])
                        dma("sp", vbuf[sl][0:16, 32, :], vd_s.ap()[4096:4112, hd * 256:(hd + 1) * 256],
                            ["vd_s"], [("vbuf", sl)])

                srot = {"n": 0}
                orot = {"n": 0}
                prot = {"n": 0}
                SKEW = 2

                jobs = []
                for u in range(12):
                    is_mla = u < 8
                    hd = u - 8
                    for (r0, r1e) in GB:
                        for cmap in ([0] if is_mla else [0, 1]):
                            oset = (4 + 2 * (orot["n"] % 2)) if is_mla else 5
                            orot["n"] += 1
                            jobs.append(dict(u=u, sl=u % 2, is_mla=is_mla, hd=hd, r0=r0, r1e=r1e, c0=QB * r0,
                                             width=QB * (r1e - r0), cmap=cmap,
                                             mi=(8 if is_mla else 2 * hd + cmap),
                                             bO=[oset, oset + 1][:(1 if is_mla else 2)], bS=(oset + 1 if is_mla else oset + 2),
                                             nsb=(4 if is_mla else 5), skew=(3 if is_mla else 4),
                                             kblocks=["meta"] + list(range(0, 4 * (r1e - 1) + 4))))

                def geom(J, kb):
                    if kb == "meta":
                        kk, k0, vblk, ra = 16, 4096, 32, J["r0"]
                    else:
                        kk, k0, vblk, ra = 128, 128 * kb, kb, max(J["r0"], kb // 4)
                    a0 = QB * (ra - J["r0"])
                    return kk, k0, vblk, ra, a0, J["width"] - a0, J["c0"] + a0

                def stage_A(J, ki):
                    kb = J["kblocks"][ki]
                    kk, k0, vblk, ra, a0, N, q0 = geom(J, kb)
                    u, sl, hd, cmap, mi = J["u"], J["sl"], J["hd"], J["cmap"], J["mi"]
                    sb_ = srot["n"] % J["nsb"]
                    srot["n"] += 1
                    pS = ps[sb_]
                    if J["is_mla"]:
                        mm(pS[0:kk, 0:N], kbuf[sl][:, 0, k0:k0 + kk], qn[:, u, q0:q0 + N], True, False,
                           [("kbuf", sl), "qn"], [PS(sb_)])
                        mm(pS[0:kk, 0:N], krT[:, k0:k0 + kk], qr[:, u, q0:q0 + N], False, True,
                           ["krT", "krT_pad", "qr", "qr_pad"], [PS(sb_)])
                    else:
                        mm(pS[0:kk, 0:N], kbuf[sl][:, cmap, k0:k0 + kk], qd[:, 2 * hd + cmap, q0:q0 + N],
                           True, True, [("kbuf", sl), "qd"], [PS(sb_)])
                    for r in range(ra, J["r1e"]):
                        cc = QB * (r - ra)
                        if kb == "meta":
                            if r == 0:
                                tt("dve", pS[0:16, cc:cc + QB], pS[0:16, cc:cc + QB],
                                   btile[0:16, mi * 6 + 5, :], ALU.add, [PS(sb_), "btile"], [PS(sb_)])
                        else:
                            jj = kb - 4 * r
                            if -1 <= jj <= 3:
                                tt("dve", pS[:, cc:cc + QB], pS[:, cc:cc + QB],
                                   btile[:, mi * 6 + jj + 1, :], ALU.add, [PS(sb_), "btile"], [PS(sb_)])
                    pi = prot["n"] % 5
                    prot["n"] += 1
                    if J["is_mla"]:
                        act(pT[pi][0:kk, 0:N], pS[0:kk, 0:N], AF.Exp, [PS(sb_)], [("pT", pi)],
                            bias=0.0, scale=mla_scale)
                    else:
                        act(pT[pi][0:kk, 0:N], pS[0:kk, 0:N], AF.Exp, [PS(sb_), "cm"], [("pT", pi)],
                            bias=cm_s[0:kk, mi:mi + 1], scale=diff_scale)
                    return pi

                def stage_B(J, ki, pi):
                    kb = J["kblocks"][ki]
                    kk, k0, vblk, ra, a0, N, q0 = geom(J, kb)
                    sl = J["sl"]
                    first = ki == 0
                    last = ki == len(J["kblocks"]) - 1
                    for d, bo in enumerate(J["bO"]):
                        mm(ps[bo][:, a0:a0 + N], vbuf[sl][0:kk, vblk, d * 128:(d + 1) * 128],
                           pT[pi][0:kk, 0:N], first, last, [("vbuf", sl), ("pT", pi)], [PS(bo)])
                    mm(ps[J["bS"]][:, a0:a0 + N], ones_bf[0:kk, :], pT[pi][0:kk, 0:N], first, last,
                       ["ones_bf", ("pT", pi)], [PS(J["bS"])])

                later = []

                def stage_F(J):
                    u, hd, cmap, c0, width, bO, bS = J["u"], J["hd"], J["cmap"], J["c0"], J["width"], J["bO"], J["bS"]
                    if not J["is_mla"]:
                        for d in range(2):
                            vcopy(Oc[:, d, 0:width], ps[bO[d]][:, 0:width], [PS(bO[d])], [("Oc", d)])
                    act(rec[:, 0:width], ps[bS][:, 0:width], AF.Ln, [PS(bS)], ["rec"])
                    act(rec[:, 0:width], rec[:, 0:width], AF.Exp, ["rec"], ["rec"], bias=0.0, scale=-1.0)
                    if J["is_mla"]:
                        tt("dve", oT[:, u, c0:c0 + width], ps[bO[0]][:, 0:width], rec[:, 0:width], ALU.mult,
                           [PS(bO[0]), "rec"], ["oT"])
                    elif cmap == 0:
                        for d in range(2):
                            tt("dve", A_s[:, d, 0:width], Oc[:, d, 0:width], rec[:, 0:width], ALU.mult,
                               [("Oc", d), "rec"], [("A", d)])
                    else:
                        for d in range(2):
                            tt("dve", tmpd[:, 0:width], Oc[:, d, 0:width], rec[:, 0:width], ALU.mult,
                               [("Oc", d), "rec"], ["tmpd"])
                            stt("dve", A_s[:, d, 0:width], tmpd[:, 0:width], lam_s[:, 1:2], A_s[:, d, 0:width],
                                ALU.mult, ALU.add, ["tmpd", "lam", ("A", d)], [("A", d)])
                            act(sqd[:, d, 0:width], A_s[:, d, 0:width], AF.Square, [("A", d)], [("sqd", d)])

                        def _f2(hd=hd, c0=c0, width=width):
                            sb_ = srot["n"] % 5
                            srot["n"] += 1
                            for d in range(2):
                                mm(ps[sb_][:, 0:width], ones_bf[:, :], sqd[:, d, 0:width], d == 0, d == 1,
                                   ["ones_bf", ("sqd", d)], [PS(sb_)])
                            rstd_from(sb_, width, 256.0, rsd[:, 0:width], "rsd")
                            for d in range(2):
                                stt("dve", oT[:, 8 + 2 * hd + d, c0:c0 + width], A_s[:, d, 0:width],
                                    gsub_s[:, d:d + 1], rsd[:, 0:width], ALU.mult, ALU.mult,
                                    [("A", d), "gsub", "rsd"], ["oT"])
                        later.append([3, _f2])

                def tick():
                    for it in list(later):
                        it[0] -= 1
                        if it[0] <= 0:
                            later.remove(it)
                            it[1]()

                load_unit(0)
                load_unit(1)
                b_started = set()
                pend = []

                def do_B():
                    Jb, kib, pib = pend.pop(0)
                    ub = Jb["u"]
                    if ub not in b_started:
                        b_started.add(ub)
                        if ub >= 1 and ub + 1 < 12:
                            load_unit(ub + 1)
                    stage_B(Jb, kib, pib)
                    if kib == len(Jb["kblocks"]) - 1:
                        stage_F(Jb)

                for J in jobs:
                    for ki in range(len(J["kblocks"])):
                        pi = stage_A(J, ki)
                        pend.append((J, ki, pi))
                        while len(pend) > J["skew"]:
                            do_B()
                        tick()
                while pend:
                    do_B()
                    tick()
                for _ in range(4):
                    tick()
                S.barrier()
        right_ctx.close()
        n2_ctx = ExitStack()
        n2T = sbt(n2_ctx, "n2T", [128, 16, NQ], BF16, side="right")
        wup_b = [sbt(n2_ctx, "wup_b%d" % i, [128, 16, 256], BF16, side="right") for i in range(3)]

        def load_wup(f):
            dma("pool", wup_b[f % 3][:, :, :], wup.ap()[f].rearrange("p (k n) -> p k n", n=256), (),
                [("wup_b", f % 3)])
        with ExitStack() as es:
            aT = sbt(es, "aT", [128, 16, NQ], F32)
            wo_b = [sbt(es, "wo_b%d" % i, [128, 2, 2048], BF16) for i in range(2)]
            sqb = [sbt(es, "sqb%d" % i, [128, 390], BF16) for i in range(3)]
            rs1 = sbt(es, "rs1", [128, NQ], F32)
            rs2 = rs1
            xqc = [sbt(es, "xqc%d" % i, [128, NQ], F32) for i in range(4)]

            def load_xq(ch):
                dma("sp", xqc[ch % 4][:], xq.ap()[:, ch, :], (), [("xqc", ch % 4)])

            def load_wo(g):
                dma("pool", wo_b[g % 2][:, :, :], wo.ap()[2 * g:2 * g + 2].rearrange("o p f -> p o f"), (),
                    [("wo_b", g % 2)])

            load_wo(0)
            load_wo(1)
            for ch in range(4):
                load_xq(ch)
            pending = []
            sqi = 0
            for oc in range(16):
                g = oc // 2
                if oc % 2 == 0 and 2 <= g + 1 < 8:
                    load_wo(g + 1)
                for gi, (c0, n) in enumerate(CG):
                    b = next_bank(0, 5)
                    for kc in range(16):
                        mm(ps[b][:, 0:n], wo_b[g % 2][:, oc % 2, kc * 128:(kc + 1) * 128], oT[:, kc, c0:c0 + n],
                           kc == 0, kc == 15, [("wo_b", g % 2), "oT"], [PS(b)])
                    for f in pending:
                        f()
                    pending = []
                    si = sqi % 3
                    sqi += 1
                    act(aT[:, oc, c0:c0 + n], ps[b][:, 0:n], AF.Copy, [PS(b)], [("aT", oc)])
                    act(sqb[si][:, 0:n], ps[b][:, 0:n], AF.Square, [PS(b)], [("sqb", si)])

                    def _f(si=si, gi=gi, n=n, oc=oc):
                        mm(ps[5 + gi][:, 0:n], ones_bf[:, :], sqb[si][:, 0:n], oc == 0, oc == 15,
                           ["ones_bf", ("sqb", si)], [PS(5 + gi)])
                    pending.append(_f)
            for f in pending:
                f()
            pending = []
            load_wup(0)
            load_wup(1)
            for gi, (c0, n) in enumerate(CG):
                rstd_from(5 + gi, n, 2048.0, rs1[:, c0:c0 + n], ("rs1", gi))
            RS1 = [("rs1", gi) for gi in range(3)]
            sqi = 0
            for ch in range(16):
                xs = ch % 4
                stt("dve", aT[:, ch, :], aT[:, ch, :], gains_s[:, 16 + ch:17 + ch], rs1[:, :], ALU.mult, ALU.mult,
                    [("aT", ch), "gains"] + RS1, [("aT", ch)])
                tt("dve", aT[:, ch, :], aT[:, ch, :], xqc[xs][:], ALU.add,
                   [("aT", ch), ("xqc", xs)], [("aT", ch)])
                if ch + 4 < 16:
                    load_xq(ch + 4)
                for gi, (c0, n) in enumerate(CG):
                    si = sqi % 3
                    sqi += 1
                    act(sqb[si][:, 0:n], aT[:, ch, c0:c0 + n], AF.Square, [("aT", ch)], [("sqb", si)])
                    mm(ps[5 + gi][:, 0:n], ones_bf[:, :], sqb[si][:, 0:n], ch == 0, ch == 15,
                       ["ones_bf", ("sqb", si)], [PS(5 + gi)])
                dma("act", h1_s.ap()[:, ch, :].rearrange("p (r t) -> p r t", t=128),
                    aT[:, ch, :].rearrange("p (r c) -> p r c", c=QB)[:, :, 2:QB], [("aT", ch)], ["h1_s"])
            for gi, (c0, n) in enumerate(CG):
                rstd_from(5 + gi, n, 2048.0, rs2[:, c0:c0 + n], ("rs1", gi))
            RS2 = [("rs1", gi) for gi in range(3)]
            for ch in range(16):
                stt("dve", n2T[:, ch, :], aT[:, ch, :], gains_s[:, 32 + ch:33 + ch],
                    rs2[:, :], ALU.mult, ALU.mult, [("aT", ch), "gains"] + RS2, [("n2T", ch)])
            S.barrier()
        oT_ctx.close()

        act_ctx = ExitStack()
        actT = sbt(act_ctx, "actT", [128, 44, 1024], BF16)
        wdn_b = [sbt(act_ctx, "wdn_b%d" % i, [128, 44, 128], BF16) for i in range(2)]

        def load_wdn(oc):
            dma("pool", wdn_b[oc % 2][:, :, :], wdn.ap()[oc].rearrange("p (k n) -> p k n", n=128), (),
                [("wdn_b", oc % 2)])
        with ExitStack() as es:
            tg = [sbt(es, "tg%d" % i, [128, 3, 128], F32) for i in range(2)]
            tv = [sbt(es, "tv%d" % i, [128, 3, 128], F32) for i in range(2)]
            sg = [sbt(es, "sg%d" % i, [128, 3, 128], F32) for i in range(2)]

            ti = 0
            for f in range(44):
                if f + 2 < 44:
                    load_wup(f + 2)
                if f == 36:
                    load_wdn(0)
                    load_wdn(1)
                for gi, (c0, n) in enumerate(CG):
                    nb = n // QB
                    r0 = GB[gi][0]
                    sl = ti % 2
                    ti += 1
                    tgs = [tg[sl], tv[sl]]
                    for part in range(2):
                        b = next_bank()
                        for kc in range(16):
                            mm(ps[b][:, 0:n], wup_b[f % 3][:, kc, part * 128:(part + 1) * 128], n2T[:, kc, c0:c0 + n],
                               kc == 0, kc == 15, [("wup_b", f % 3), "n2T"], [PS(b)])
                        pv = ps[b][:, 0:n].rearrange("p (r c) -> p r c", c=QB)
                        cb = f * 8 + part * 4
                        tk = ("t", part, sl)
                        act(tgs[part][:, 0:nb, :], pv[:, :, 2:QB], AF.Identity, [PS(b), "convp"], [tk],
                            bias=convp_s[:, cb + 3:cb + 4], scale=convp_s[:, cb + 2:cb + 3])
                        stt("dve", tgs[part][:, 0:nb, :], pv[:, :, 1:QB - 1], convp_s[:, cb + 1:cb + 2],
                            tgs[part][:, 0:nb, :], ALU.mult, ALU.add, [PS(b), "convp", tk], [tk])
                        stt("dve", tgs[part][:, 0:nb, :], pv[:, :, 0:QB - 2], convp_s[:, cb:cb + 1],
                            tgs[part][:, 0:nb, :], ALU.mult, ALU.add, [PS(b), "convp", tk], [tk])
                    act(sg[sl][:, 0:nb, :], tg[sl][:, 0:nb, :], AF.Silu, [("t", 0, sl)], [("sg", sl)])
                    tt("pool", actT[:, f, r0 * 128:(r0 + nb) * 128].rearrange("p (r t) -> p r t", t=128),
                       sg[sl][:, 0:nb, :], tv[sl][:, 0:nb, :], ALU.mult, [("sg", sl), ("t", 1, sl)], [("actT", f)])
            S.barrier()
        n2_ctx.close()

        with ExitStack() as es:
            fT = sbt(es, "fT", [128, 16, 1024], F32)
            sqb = [sbt(es, "sqc%d" % i, [128, 512], BF16) for i in range(3)]
            rs3 = sbt(es, "rs3", [128, 1024], F32)
            h1c = [sbt(es, "h1c%d" % i, [128, 1024], F32) for i in range(3)]
            ACTK = [("actT", f) for f in range(44)]

            pending = []
            sqi = 0
            for oc in range(16):
                if 2 <= oc + 1 < 16:
                    load_wdn(oc + 1)
                for tgi in range(2):
                    b = next_bank(0, 6)
                    for kc in range(44):
                        mm(ps[b][:, 0:512], wdn_b[oc % 2][:, kc, :], actT[:, kc, tgi * 512:(tgi + 1) * 512],
                           kc == 0, kc == 43, [("wdn_b", oc % 2)] + (ACTK if kc == 0 else []), [PS(b)])
                    for f_ in pending:
                        f_()
                    pending = []
                    si = sqi % 3
                    sqi += 1
                    act(fT[:, oc, tgi * 512:(tgi + 1) * 512], ps[b][:, 0:512], AF.Copy, [PS(b)], [("fT", oc)])
                    act(sqb[si][:, :], ps[b][:, 0:512], AF.Square, [PS(b)], [("sqb", si)])

                    def _f(si=si, tgi=tgi, oc=oc):
                        mm(ps[6 + tgi][:, 0:512], ones_bf[:, :], sqb[si][:, :], oc == 0, oc == 15,
                           ["ones_bf", ("sqb", si)], [PS(6 + tgi)])
                    pending.append(_f)
            for f_ in pending:
                f_()
            for tgi in range(2):
                rstd_from(6 + tgi, 512, 2048.0, rs3[:, tgi * 512:(tgi + 1) * 512], ("rs3", tgi))
            RS3 = [("rs3", 0), ("rs3", 1)]
            outs = []
            for ch in range(16):
                xs = ch % 3
                dma("sp", h1c[xs][:], h1_s.ap()[:, ch, :], ["h1_s"], [("h1c", xs)])
                stt("dve", fT[:, ch, :], fT[:, ch, :], gains_s[:, 48 + ch:49 + ch], rs3[:, :], ALU.mult, ALU.mult,
                    [("fT", ch), "gains"] + RS3, [("fT", ch)])
                tt("dve", fT[:, ch, :], fT[:, ch, :], h1c[xs][:], ALU.add,
                   [("fT", ch), ("h1c", xs)], [("fT", ch)])
                dma("act", out.ap()[:, ch, :], fT[:, ch, :], [("fT", ch)], [("out", ch)])
                outs.append(("out", ch))
            S.add("sp", None, reads=outs)
        act_ctx.close()

        S.finalize()
        sems = {s: top.enter_context(nc.semaphore("s_" + s)) for s in S.streams}
        block = top.enter_context(nc.Block())
        block.tensor(lambda e: S.emit("pe", e, sems))
        block.scalar(lambda e: S.emit("act", e, sems))
        block.vector(lambda e: S.emit("dve", e, sems))
        block.gpsimd(lambda e: S.emit("pool", e, sems))
        block.sync(lambda e: S.emit("sp", e, sems))
    return nc


def _t5_bucket(n):
    n = np.maximum(n, 0)
    nf = np.maximum(n, 1).astype(np.float32)
    large = 16 + (np.log(nf / np.float32(16.0)) / np.float32(math.log(8.0)) * np.float32(16.0)).astype(np.int32)
    large = np.minimum(large, 31)
    return np.where(n < 16, n, large)


def _fm(w, ncols=None):
    K, N = w.shape
    return np.ascontiguousarray(w.reshape(K // 128, 128, N).transpose(1, 0, 2))


def _prep_shared(inp):
    f = np.float32
    w_in = inp["w_in"][0]
    cq, ckv, kr, qdw, kdw, vdw = np.split(w_in, np.cumsum([512, 256, 64, 1024, 1024])[:], axis=1)
    kr_sw = np.concatenate([kr[:, 32:64], kr[:, 0:32]], axis=1)
    sh = {}
    sh["wkv"] = _fm(np.concatenate([ckv, kr, kr_sw, kdw, vdw], axis=1))
    sh["wq"] = _fm(np.concatenate([cq, qdw], axis=1))
    w_ukv = inp["w_ukv"][0].reshape(256, 8, 256)
    sh["wukv"] = _fm(np.concatenate([w_ukv[:, :, 0:128].reshape(256, 1024), w_ukv[:, :, 128:256].reshape(256, 1024)], axis=1))
    w_uq = inp["w_uq"][0].reshape(512, 8, 192)
    wuq = np.concatenate([w_uq[:, :, 0:128], w_uq[:, :, 128:192], w_uq[:, :, 160:192], w_uq[:, :, 128:160]], axis=2)
    sh["wuq"] = _fm(wuq.reshape(512, 2048))
    w_o = inp["w_o"][0]
    sh["wo"] = np.ascontiguousarray(w_o.reshape(16, 128, 16, 128).transpose(2, 1, 0, 3).reshape(16, 128, 2048))
    w_up = inp["w_up"][0]
    g = w_up[:, :5632].reshape(16, 128, 44, 128)
    v = w_up[:, 5632:].reshape(16, 128, 44, 128)
    gv = np.concatenate([g, v], axis=3)
    sh["wup"] = np.ascontiguousarray(gv.transpose(2, 1, 0, 3).reshape(44, 128, 4096))
    w_dn = inp["w_down"][0]
    sh["wdn"] = np.ascontiguousarray(w_dn.reshape(44, 128, 16, 128).transpose(2, 1, 0, 3).reshape(16, 128, 5632))
    cols = []
    for k in ("g_attn_pre", "g_attn_post", "g_ffn_pre", "g_ffn_post"):
        cols.append(inp[k][0].reshape(16, 128).T)
    cols.append(inp["g_cq"][0].reshape(4, 128).T)
    cols.append(inp["g_ckv"][0].reshape(2, 128).T)
    cols.append(inp["g_diff_sub"][0].reshape(2, 128).T)
    sh["gains"] = np.ascontiguousarray(np.concatenate(cols, axis=1).astype(f))
    cw = inp["conv_w"][0]
    cb = inp["conv_b"][0]
    cp = np.zeros((128, 44, 2, 4), f)
    for part in range(2):
        o = part * 5632
        for j in range(3):
            cp[:, :, part, j] = cw[j, o:o + 5632].reshape(44, 128).T
        cp[:, :, part, 3] = cb[o:o + 5632].reshape(44, 128).T
    sh["convp"] = np.ascontiguousarray(cp.reshape(128, 352))
    sh["rb"] = np.ascontiguousarray(inp["rel_bias"].astype(f))
    sh["rb31"] = np.ascontiguousarray(np.repeat(inp["rel_bias"][31:32, :], 128, axis=0).astype(f))
    sh["lamv"] = np.ascontiguousarray(np.stack([inp["lambda_q1"][0], inp["lambda_q2"][0],
                                                inp["lambda_k1"][0], inp["lambda_k2"][0]], axis=1).astype(f))
    inv = 10000.0 ** (-np.arange(32, dtype=np.float64) / 32.0)
    posk = np.concatenate([16 + np.arange(4096), np.arange(16)]).astype(np.float64)
    ang = posk[None, :] * inv[:, None]
    sh["cosk"] = np.ascontiguousarray(np.concatenate([np.cos(ang), np.cos(ang)], axis=0).astype(f))
    sh["sink"] = np.ascontiguousarray(np.concatenate([-np.sin(ang), np.sin(ang)], axis=0).astype(f))
    sh["_inv"] = inv
    return sh


def _prep_core(inp, sh, c, xkv_b):
    f = np.float32
    b, i = c // 4, c % 4
    m = {k: v for k, v in sh.items() if not k.startswith("_")}
    m["xkv"] = xkv_b[b]
    seq = np.concatenate([inp["meta_tokens"].astype(f), inp["x"][b]], axis=0)
    pos = np.concatenate([16 + 128 * (4 * r + i) - 2 + np.arange(QB) for r in range(8)])
    xo = seq[pos]
    m["xq"] = np.ascontiguousarray(xo.T.reshape(16, 128, NQ).transpose(1, 0, 2))
    ang = pos.astype(np.float64)[None, :] * sh["_inv"][:, None]
    m["cosq"] = np.ascontiguousarray(np.concatenate([np.cos(ang), np.cos(ang)], axis=0).astype(f))
    m["sinq"] = np.ascontiguousarray(np.concatenate([-np.sin(ang), np.sin(ang)], axis=0).astype(f))
    u = np.arange(769) - 513
    rrel = u + 128 * i
    ohm = np.zeros((33, 769), f)
    bk = _t5_bucket(rrel)
    for idx in range(769):
        if rrel[idx] < 0:
            ohm[32, idx] = 1.0
        else:
            ohm[bk[idx], idx] = 1.0
    m["oh"] = ohm
    return m


_NC_CACHE = {}


def kernel(**inputs):
    inp = {k: np.asarray(v) for k, v in inputs.items()}
    if "nc" not in _NC_CACHE:
        _NC_CACHE["nc"] = build_nc()
    nc = _NC_CACHE["nc"]
    sh = _prep_shared(inp)
    xkv_b = []
    for b in range(2):
        seqk = np.concatenate([inp["x"][b], inp["meta_tokens"].astype(np.float32)], axis=0)
        xkv_b.append(np.ascontiguousarray(seqk.T.reshape(16, 128, TKV).transpose(1, 0, 2)))
    in_maps = [_prep_core(inp, sh, c, xkv_b) for c in range(NCORES)]
    res = run_bass_kernel_spmd(nc, in_maps, core_ids=list(range(NCORES)))
    outp = np.zeros((2, 4096, 2048), np.float32)
    for c in range(NCORES):
        b, i = c // 4, c % 4
        o = np.asarray(res.results[c]["out"]).reshape(128, 16, 8, 128)
        for r in range(8):
            g = 4 * r + i
            outp[b, 128 * g:128 * (g + 1), :] = o[:, :, r, :].transpose(2, 1, 0).reshape(128, 2048)
    return outp
```

```python
import math
from contextlib import ExitStack

import numpy as np
import concourse.bass as bass
import concourse.mybir as mybir
from concourse.bass_utils import run_bass_kernel_spmd

F32 = mybir.dt.float32
BF16 = mybir.dt.bfloat16
ALU = mybir.AluOpType
AF = mybir.ActivationFunctionType

NCORES = 8
TKV = 4112
NQ = 1040
QB = 130
EPS = 1e-6
NEG = -30000.0
CG = [(0, 390), (390, 390), (780, 260)]
GB = [(0, 3), (3, 6), (6, 8)]


class _Op:
    __slots__ = ("stream", "eng", "fn", "deps", "signal", "count")


class Sched:
    ENGS = ("pe", "act", "dve", "pool", "sp")

    NDMASEM = 16

    def __init__(self, same_engine_sync=True):
        self.q = {e: [] for e in self.ENGS}
        self.lastw = {}
        self.readers = {}
        self.last_real = {}
        self.same_engine_sync = same_engine_sync
        self.dma_rr = {}

    def add(self, eng, fn, reads=(), writes=(), dma=False):
        op = _Op()
        op.eng = eng
        if dma:
            k = self.dma_rr.get(eng, 0)
            self.dma_rr[eng] = k + 1
            op.stream = "dma_%s_%d" % (eng, k % self.NDMASEM)
        else:
            op.stream = eng
        op.fn = fn
        op.signal = bool(dma)
        op.count = 0
        deps = []
        seen = set()

        def _add(d):
            if d is None or id(d) in seen:
                return
            seen.add(id(d))
            if d.stream == op.stream and not dma:
                if op.stream == "pe" or not self.same_engine_sync:
                    return
            deps.append(d)

        if dma:
            _add(self.last_real.get(op.stream))
        for k in reads:
            _add(self.lastw.get(k))
        for k in writes:
            _add(self.lastw.get(k))
            for r in self.readers.get(k, {}).values():
                _add(r)
        op.deps = deps
        for d in deps:
            d.signal = True
        for k in reads:
            self.readers.setdefault(k, {})[op.stream] = op
        for k in writes:
            self.lastw[k] = op
            self.readers[k] = {}
        self.q[eng].append(op)
        self.last_real[op.stream] = op
        return op

    def barrier(self):
        lasts = list(self.last_real.values())
        for e in self.ENGS:
            op = _Op()
            op.eng = e
            op.stream = e
            op.fn = None
            op.signal = False
            op.count = 0
            op.deps = list(lasts)
            for d in lasts:
                d.signal = True
            self.q[e].append(op)
        self.lastw = {}
        self.readers = {}

    def finalize(self):
        cnt = {}
        for e in self.ENGS:
            for op in self.q[e]:
                if op.signal:
                    cnt[op.stream] = cnt.get(op.stream, 0) + 1
                    op.count = cnt[op.stream]
        for s, c in cnt.items():
            assert c * (16 if s.startswith("dma_") else 1) < 60000, (s, c)
        self.totals = cnt
        self.streams = sorted(set(list(cnt.keys()) + list(self.ENGS)))

    def emit(self, eng_name, engine, sems):
        waited = {}
        for op in self.q[eng_name]:
            for d in op.deps:
                val = d.count * (16 if d.stream.startswith("dma_") else 1)
                if waited.get(d.stream, 0) < val:
                    engine.wait_ge(sems[d.stream], val)
                    waited[d.stream] = val
            if op.fn is None:
                continue
            ins = op.fn(engine)
            if op.signal:
                ins.then_inc(sems[op.stream], 16 if op.stream.startswith("dma_") else 1)


def build_nc():
    nc = bass.Bass("TRN2", target_bir_lowering=False)
    S = Sched()

    def din(name, shape, dt=F32):
        return nc.dram_tensor(name, shape, dt, kind="ExternalInput")

    xkv = din("xkv", [128, 16, TKV])
    xq = din("xq", [128, 16, NQ])
    wkv = din("wkv", [128, 16, 2432])
    wukv = din("wukv", [128, 2, 2048])
    wq = din("wq", [128, 16, 1536])
    wuq = din("wuq", [128, 4, 2048])
    wo = din("wo", [16, 128, 2048])
    wup = din("wup", [44, 128, 4096])
    wdn = din("wdn", [16, 128, 5632])
    gains = din("gains", [128, 72])
    convp = din("convp", [128, 352])
    rb = din("rb", [32, 8])
    rb31 = din("rb31", [128, 8])
    lamv = din("lamv", [128, 4])
    oh = din("oh", [33, 769])
    cosq = din("cosq", [64, NQ])
    sinq = din("sinq", [64, NQ])
    cosk = din("cosk", [64, TKV])
    sink = din("sink", [64, TKV])
    out = nc.dram_tensor("out", [128, 16, 1024], F32, kind="ExternalOutput")

    kn_s = nc.dram_tensor("kn_s", [8, 128, TKV], BF16)
    kd_s = nc.dram_tensor("kd_s", [8, 128, TKV], BF16)
    kr_s = nc.dram_tensor("kr_s", [64, TKV], BF16)
    vm_s = nc.dram_tensor("vm_s", [TKV, 1024], BF16)
    vd_s = nc.dram_tensor("vd_s", [TKV, 1024], BF16)
    tscr = nc.dram_tensor("tscr", [9, 128, 769], F32)
    h1_s = nc.dram_tensor("h1_s", [128, 16, 1024], F32)

    def mm(o, lhsT, rhs, start, stop, reads, writes):
        S.add("pe", lambda e: e.matmul(o, lhsT=lhsT, rhs=rhs, start=start, stop=stop), reads, writes)

    def act(o, i, func, reads, writes, bias=0.0, scale=1.0):
        S.add("act", lambda e: e.activation(out=o, in_=i, func=func, bias=bias, scale=scale), reads, writes)

    def tt(eng, o, a, b, op, reads, writes):
        S.add(eng, lambda e: e.tensor_tensor(out=o, in0=a, in1=b, op=op), reads, writes)

    def ts(eng, o, a, s1, s2, op0, op1, reads, writes):
        if op1 is None:
            S.add(eng, lambda e: e.tensor_scalar(out=o, in0=a, scalar1=s1, scalar2=None, op0=op0), reads, writes)
        else:
            S.add(eng, lambda e: e.tensor_scalar(out=o, in0=a, scalar1=s1, scalar2=s2, op0=op0, op1=op1), reads, writes)

    def stt(eng, o, a, sc, b, op0, op1, reads, writes):
        S.add(eng, lambda e: e.scalar_tensor_tensor(out=o, in0=a, scalar=sc, in1=b, op0=op0, op1=op1), reads, writes)

    def recip(o, i, reads, writes):
        S.add("dve", lambda e: e.reciprocal(out=o, in_=i), reads, writes)

    def vcopy(o, i, reads, writes):
        S.add("dve", lambda e: e.tensor_copy(out=o, in_=i), reads, writes)

    def memset(eng, o, v, writes):
        S.add(eng, lambda e: e.memset(o, v), (), writes)

    def dma(eng, o, i, reads, writes):
        S.add(eng, lambda e: e.dma_start(out=o, in_=i), reads, writes, dma=True)

    evac_flip = [0]

    def evac(o, i, reads, writes):
        evac_flip[0] ^= 1
        if evac_flip[0]:
            act(o, i, AF.Copy, reads, writes)
        else:
            vcopy(o, i, reads, writes)

    with ExitStack() as top:
        def sbt(es, name, shape, dt, side=None):
            if side is None:
                return es.enter_context(nc.sbuf_tensor(name, shape, dt))
            return es.enter_context(nc.sbuf_tensor(name, shape, dt, side=side))

        ps = [top.enter_context(nc.psum_tensor("ps%d" % b, [128, 512], F32)) for b in range(8)]

        rot = {"n": 0}

        def next_bank(lo=0, hi=8):
            b = lo + rot["n"] % (hi - lo)
            rot["n"] += 1
            return b

        def PS(b):
            return ("ps", b)

        gains_s = sbt(top, "gains_s", [128, 72], F32)
        convp_s = sbt(top, "convp_s", [128, 352], F32)
        cm_s = sbt(top, "cm_s", [128, 8], F32)
        ones_bf = sbt(top, "ones_bf", [128, 128], BF16)
        ones_f = sbt(top, "ones_f", [128, 128], F32)
        eps_s = sbt(top, "eps_s", [128, 1], F32)
        lam_s = sbt(top, "lam_s", [128, 4], F32)
        gsub_s = sbt(top, "gsub_s", [128, 2], F32)
        right_ctx = ExitStack()
        btile = sbt(right_ctx, "btile", [128, 9 * 6, QB], F32, side="right")

        dma("sp", gains_s[:], gains.ap(), (), ["gains"])
        dma("sp", convp_s[:], convp.ap(), (), ["convp"])
        dma("sp", cm_s[:], rb31.ap(), (), ["cm"])
        memset("dve", ones_bf[:], 1.0, ["ones_bf"])
        memset("dve", ones_f[:], 1.0, ["ones_f"])
        memset("dve", eps_s[:], EPS, ["eps"])

        setup_ctx = ExitStack()
        lamv_s = sbt(setup_ctx, "lamv_s", [128, 4], F32, side="right")
        prod_s = sbt(setup_ctx, "prod_s", [128, 2], F32, side="right")
        e_s = sbt(setup_ctx, "e_s", [128, 2], F32, side="right")
        rb_s = sbt(setup_ctx, "rb_s", [32, 8], F32, side="right")
        oh_s = sbt(setup_ctx, "oh_s", [33, 769], F32, side="right")
        lhsb = sbt(setup_ctx, "lhsb", [33, 9, 128], F32, side="right")
        T_s = sbt(setup_ctx, "T_s", [128, 769], F32, side="right")

        def emit_setup():
            dma("sp", lamv_s[:], lamv.ap(), (), ["lamv"])
            dma("sp", rb_s[:], rb.ap(), (), ["rb"])
            dma("sp", oh_s[:], oh.ap(), (), ["oh"])
            tt("dve", prod_s[:], lamv_s[:, 0:2], lamv_s[:, 2:4], ALU.mult, ["lamv"], ["prod"])
            b0 = next_bank()
            mm(ps[b0][:, 0:2], ones_f[:, :], prod_s[:, :], True, True, ["ones_f", "prod"], [PS(b0)])
            act(e_s[:], ps[b0][:, 0:2], AF.Exp, [PS(b0)], ["e_s"])
            tt("dve", lam_s[:, 0:1], e_s[:, 0:1], e_s[:, 1:2], ALU.subtract, ["e_s"], ["lam"])
            ts("dve", lam_s[:, 0:1], lam_s[:, 0:1], 0.2, None, ALU.add, None, ["lam"], ["lam"])
            ts("dve", lam_s[:, 1:2], lam_s[:, 0:1], -1.0, None, ALU.mult, None, ["lam"], ["lam"])
            ts("dve", gsub_s[:], gains_s[:, 70:72], 0.8, None, ALU.mult, None, ["gains"], ["gsub"])
            tt("dve", rb_s[:], rb_s[:], cm_s[0:32, :], ALU.subtract, ["rb", "cm"], ["rb"])
            ts("dve", rb_s[:], rb_s[:], math.sqrt(128.0), None, ALU.mult, None, ["rb"], ["rb"])
            memset("dve", lhsb[:], 1.0, ["lhsb"])
            for m in range(8):
                ts("dve", lhsb[0:32, m, :], lhsb[0:32, m, :], rb_s[:, m:m + 1], None, ALU.mult, None,
                   ["lhsb", "rb"], ["lhsb"])
            memset("dve", lhsb[0:32, 8, :], 0.0, ["lhsb"])
            memset("dve", lhsb[32:33, :, :], NEG, ["lhsb"])

        def emit_setup_map(m):
            ba, bb = next_bank(), next_bank()
            mm(ps[ba][:, 0:512], lhsb[:, m, :], oh_s[:, 0:512], True, True, ["lhsb", "oh"], [PS(ba)])
            mm(ps[bb][:, 0:257], lhsb[:, m, :], oh_s[:, 512:769], True, True, ["lhsb", "oh"], [PS(bb)])
            vcopy(T_s[:, 0:512], ps[ba][:, 0:512], [PS(ba)], ["T_s"])
            vcopy(T_s[:, 512:769], ps[bb][:, 0:257], [PS(bb)], ["T_s"])
            dma("pool", tscr.ap()[m], T_s[:], ["T_s"], [("tscr", m)])
            for jx in range(5):
                jj = jx - 1
                src = bass.AP(tensor=tscr, offset=m * 128 * 769 + 511 - 128 * jj, ap=[[768, 128], [1, QB]])
                dma("pool", btile[:, m * 6 + jx, :], src, [("tscr", m)], ["btile"])
            src = bass.AP(tensor=tscr, offset=m * 128 * 769 + 527, ap=[[768, 16], [1, QB]])
            dma("pool", btile[0:16, m * 6 + 5, :], src, [("tscr", m)], ["btile"])

        def rstd_from(bank, n, nfeat, rs_ap, rs_key):
            act(rs_ap, ps[bank][:, 0:n], AF.Ln, [PS(bank), "eps"], [rs_key], bias=eps_s[:, 0:1], scale=1.0 / nfeat)
            act(rs_ap, rs_ap, AF.Exp, [rs_key], [rs_key], bias=0.0, scale=-0.5)

        with ExitStack() as es:
            wkv_s = sbt(es, "wkv_s", [128, 16, 2432], BF16)
            wukv_s = sbt(es, "wukv_s", [128, 2, 2048], BF16)
            xg = sbt(es, "xg", [128, 16, 256], F32)
            sq = sbt(es, "sq", [128, 16, 256], BF16)
            nT = [sbt(es, "nT%d" % i, [128, 16, 256], BF16) for i in range(2)]
            rs = sbt(es, "rs", [128, 256], F32)
            rs2 = sbt(es, "rs2", [128, 256], F32)
            ck = [sbt(es, "ck%d" % i, [64, 256], F32) for i in range(2)]
            sk = [sbt(es, "sk%d" % i, [64, 256], F32) for i in range(2)]
            r1 = sbt(es, "r1", [64, 256], F32)
            r2 = sbt(es, "r2", [64, 256], F32)
            kr_o = sbt(es, "kr_o", [64, 256], BF16)
            kd_o2 = [sbt(es, "kd_o%d" % i, [128, 8, 256], BF16) for i in range(2)]
            kn_o = sbt(es, "kn_o", [128, 8, 256], BF16)
            vd_o = sbt(es, "vd_o", [128, 2, 1024], BF16)
            vm_o = sbt(es, "vm_o", [128, 2, 1024], BF16)
            ckv_f = sbt(es, "ckv_f", [128, 2, 256], F32)
            ckv_sq = sbt(es, "ckv_sq", [128, 2, 256], BF16)
            ckvn2 = [sbt(es, "ckvn%d" % i, [128, 2, 256], BF16) for i in range(2)]

            WP = [(0, 384), (384, 896), (896, 1408), (1408, 1920), (1920, 2432)]
            for pi_, (a, b_) in enumerate(WP):
                dma("pool", wkv_s[:, :, a:b_], wkv.ap()[:, :, a:b_], ([("wkv", pi_ - 1)] if pi_ else ()),
                    [("wkv", pi_)])
            dma("pool", wukv_s[:], wukv.ap(), [("wkv", len(WP) - 1)], ["wukv"])

            def WK(col):
                for pi_, (a, b_) in enumerate(WP):
                    if a <= col < b_:
                        return ("wkv", pi_)

            groups = [(t0, 256) for t0 in range(0, 4096, 256)] + [(4096, 16)]

            def load_x(gi):
                t0, n = groups[gi]
                sl = gi % 2
                dma("sp", xg[:, :, 0:n], xkv.ap()[:, :, t0:t0 + n], (), ["xg"])
                dma("sp", ck[sl][:, 0:n], cosk.ap()[:, t0:t0 + n], (), [("ck", sl)])
                dma("sp", sk[sl][:, 0:n], sink.ap()[:, t0:t0 + n], (), [("sk", sl)])

            def pre_square(gi):
                t0, n = groups[gi]
                act(sq[:, :, 0:n], xg[:, :, 0:n], AF.Square, ["xg"], ["sq"])

            def prologue(gi):
                t0, n = groups[gi]
                sl = gi % 2
                nTc = nT[sl]
                b = next_bank()
                for kc in range(16):
                    mm(ps[b][:, 0:n], ones_bf[:, :], sq[:, kc, 0:n], kc == 0, kc == 15, ["ones_bf", "sq"], [PS(b)])
                rstd_from(b, n, 2048.0, rs[:, 0:n], "rs")
                for kc in range(16):
                    stt("dve", nTc[:, kc, 0:n], xg[:, kc, 0:n], gains_s[:, kc:kc + 1], rs[:, 0:n], ALU.mult, ALU.mult,
                        ["xg", "gains", "rs"], [("nT", sl)])

            def fm_chunk(gi, col0, ncol_out, pbank=None, pcol=0):
                t0, n = groups[gi]
                sl = gi % 2
                b = next_bank() if pbank is None else pbank
                for kc in range(16):
                    mm(ps[b][0:ncol_out, pcol:pcol + n], wkv_s[:, kc, col0:col0 + ncol_out], nT[sl][:, kc, 0:n],
                       kc == 0, kc == 15, [WK(col0), ("nT", sl)], [PS(b)])
                return b

            def body1(gi):
                t0, n = groups[gi]
                sl = gi % 2
                for c in range(2):
                    b = fm_chunk(gi, c * 128, 128)
                    act(ckv_f[:, c, 0:n], ps[b][:, 0:n], AF.Copy, [PS(b)], [("ckv_f", c)])
                    act(ckv_sq[:, c, 0:n], ps[b][:, 0:n], AF.Square, [PS(b)], [("ckv_sq", c)])
                b = fm_chunk(gi, 256, 64)
                fm_chunk(gi, 320, 64, pbank=b, pcol=256)
                tt("dve", r1[:, 0:n], ps[b][0:64, 0:n], ck[sl][:, 0:n], ALU.mult, [PS(b), ("ck", sl)], ["r1"])
                tt("dve", r2[:, 0:n], ps[b][0:64, 256:256 + n], sk[sl][:, 0:n], ALU.mult, [PS(b), ("sk", sl)], ["r2"])
                tt("dve", kr_o[:, 0:n], r1[:, 0:n], r2[:, 0:n], ALU.add, ["r1", "r2"], ["kr_o"])
                dma("sp", kr_s.ap()[:, t0:t0 + n], kr_o[:, 0:n], ["kr_o"], ["kr_s"])
                for m in range(4):
                    b = fm_chunk(gi, 384 + m * 128, 128)
                    evac(kd_o2[sl][:, m, 0:n], ps[b][:, 0:n], [PS(b)], [("kd_o", sl)])
                b = next_bank()
                for c in range(2):
                    mm(ps[b][:, 0:n], ones_bf[:, :], ckv_sq[:, c, 0:n], c == 0, c == 1,
                       ["ones_bf", ("ckv_sq", c)], [PS(b)])
                rstd_from(b, n, 256.0, rs2[:, 0:n], "rs2")
                for c in range(2):
                    stt("dve", ckvn2[sl][:, c, 0:n], ckv_f[:, c, 0:n], gains_s[:, 68 + c:69 + c], rs2[:, 0:n],
                        ALU.mult, ALU.mult, [("ckv_f", c), "gains", "rs2"], [("ckvn", sl)])

            def body2(gi):
                t0, n = groups[gi]
                sl = gi % 2
                nTc = nT[sl]
                for m in range(4, 8):
                    b = fm_chunk(gi, 384 + m * 128, 128)
                    evac(kd_o2[sl][:, m, 0:n], ps[b][:, 0:n], [PS(b)], [("kd_o", sl)])
                dma("sp", kd_s.ap()[:, :, t0:t0 + n].rearrange("m p t -> p m t"), kd_o2[sl][:, :, 0:n],
                    [("kd_o", sl)], ["kd_s"])
                tbs = [(0, 128), (128, 128)] if n == 256 else [(0, n)]
                for ti, (o0, tn) in enumerate(tbs):
                    for half in range(2):
                        b = next_bank()
                        cc0 = 1408 + half * 512
                        for kc in range(16):
                            mm(ps[b][0:tn, 0:512], nTc[:, kc, o0:o0 + tn], wkv_s[:, kc, cc0:cc0 + 512],
                               kc == 0, kc == 15, [WK(cc0), ("nT", sl)], [PS(b)])
                        evac(vd_o[0:tn, ti, half * 512:(half + 1) * 512], ps[b][0:tn, 0:512], [PS(b)], [("vd_o", ti)])
                    dma("sp", vd_s.ap()[t0 + o0:t0 + o0 + tn, :], vd_o[0:tn, ti, :], [("vd_o", ti)], ["vd_s"])
                for h in range(8):
                    b = next_bank()
                    for c in range(2):
                        mm(ps[b][:, 0:n], wukv_s[:, c, h * 128:(h + 1) * 128], ckvn2[sl][:, c, 0:n], c == 0, c == 1,
                           ["wukv", ("ckvn", sl)], [PS(b)])
                    evac(kn_o[:, h, 0:n], ps[b][:, 0:n], [PS(b)], ["kn_o"])
                dma("sp", kn_s.ap()[:, :, t0:t0 + n].rearrange("m p t -> p m t"), kn_o[:, :, 0:n], ["kn_o"], ["kn_s"])
                for ti, (o0, tn) in enumerate(tbs):
                    for half in range(2):
                        b = next_bank()
                        for c in range(2):
                            mm(ps[b][0:tn, 0:512], ckvn2[sl][:, c, o0:o0 + tn],
                               wukv_s[:, c, 1024 + half * 512:1024 + (half + 1) * 512], c == 0, c == 1,
                               ["wukv", ("ckvn", sl)], [PS(b)])
                        evac(vm_o[0:tn, ti, half * 512:(half + 1) * 512], ps[b][0:tn, 0:512], [PS(b)], [("vm_o", ti)])
                    dma("sp", vm_s.ap()[t0 + o0:t0 + o0 + tn, :], vm_o[0:tn, ti, :], [("vm_o", ti)], ["vm_s"])

            emit_setup()
            load_x(0)
            pre_square(0)
            prologue(0)
            load_x(1)
            for gi in range(len(groups)):
                if gi == 0:
                    body1(0)
                    pre_square(1)
                    prologue(1)
                    load_x(2)
                    body1(1)
                    body2(0)
                    continue
                if gi + 1 < len(groups):
                    pre_square(gi + 1)
                if gi >= 2:
                    body1(gi)
                if gi + 1 < len(groups):
                    prologue(gi + 1)
                    if gi + 2 < len(groups):
                        load_x(gi + 2)
                if 1 <= gi <= 9:
                    emit_setup_map(gi - 1)
                body2(gi)
            S.barrier()
        setup_ctx.close()


        with ExitStack() as esq:
            qn = sbt(esq, "qn", [128, 8, NQ], BF16, side="right")
            qr = sbt(esq, "qr", [128, 8, NQ], BF16, side="right")
            qd = sbt(esq, "qd", [128, 8, NQ], BF16, side="right")

            with ExitStack() as es:
                wq_s = sbt(es, "wq_s", [128, 16, 1536], BF16)
                wuq_s = sbt(es, "wuq_s", [128, 4, 2048], BF16)
                xg = sbt(es, "xgq", [128, 16, 260], F32)
                sq = sbt(es, "sqq", [128, 16, 260], BF16)
                nTq2 = [sbt(es, "nTq%d" % i, [128, 16, 260], BF16) for i in range(2)]
                rs = sbt(es, "rsq", [128, 260], F32)
                rs2 = sbt(es, "rs2q", [128, 260], F32)
                cq_f = sbt(es, "cq_f", [128, 4, 260], F32)
                cq_sq = sbt(es, "cq_sq", [128, 4, 260], BF16)
                cqn = sbt(es, "cqn", [128, 4, 260], BF16)
                cq_s = [sbt(es, "cosq_s%d" % i, [64, 260], F32) for i in range(2)]
                sq_s = [sbt(es, "sinq_s%d" % i, [64, 260], F32) for i in range(2)]
                r1 = sbt(es, "r1q", [64, 260], F32)
                r2 = sbt(es, "r2q", [64, 260], F32)
                for q, (a_, b_) in enumerate([(0, 512), (512, 1024), (1024, 1536)]):
                    dma("pool", wq_s[:, :, a_:b_], wq.ap()[:, :, a_:b_], ([("wq", q - 1)] if q else ()), [("wq", q)])
                dma("pool", wuq_s[:], wuq.ap(), [("wq", 2)], ["wuq"])
                memset("pool", qr[64:128, :, :], 0.0, ["qr_pad"])
                WQ = []
                QG = [(0, 260), (260, 260), (520, 260), (780, 260)]

                def x_load(gi):
                    c0, n = QG[gi]
                    dma("sp", xg[:, :, 0:n], xq.ap()[:, :, c0:c0 + n], (), ["xg"])

                def cs_load(gi):
                    c0, n = QG[gi]
                    dma("sp", cq_s[gi % 2][:, 0:n], cosq.ap()[:, c0:c0 + n], (), [("cosq", gi % 2)])
                    dma("sp", sq_s[gi % 2][:, 0:n], sinq.ap()[:, c0:c0 + n], (), [("sinq", gi % 2)])

                def q_sq(gi):
                    c0, n = QG[gi]
                    act(sq[:, :, 0:n], xg[:, :, 0:n], AF.Square, ["xg"], ["sq"])

                def q_pro(gi):
                    c0, n = QG[gi]
                    b = next_bank()
                    for kc in range(16):
                        mm(ps[b][:, 0:n], ones_bf[:, :], sq[:, kc, 0:n], kc == 0, kc == 15, ["ones_bf", "sq"], [PS(b)])
                    rstd_from(b, n, 2048.0, rs[:, 0:n], "rs")
                    for kc in range(16):
                        stt("dve", nTq2[gi % 2][:, kc, 0:n], xg[:, kc, 0:n], gains_s[:, kc:kc + 1], rs[:, 0:n],
                            ALU.mult, ALU.mult, ["xg", "gains", "rs"], [("nTq", gi % 2)])

                def q_body_a(gi):
                    c0, n = QG[gi]
                    for c in range(4):
                        b = next_bank()
                        for kc in range(16):
                            mm(ps[b][:, 0:n], wq_s[:, kc, c * 128:(c + 1) * 128], nTq2[gi % 2][:, kc, 0:n],
                               kc == 0, kc == 15, [("wq", 0), ("nTq", gi % 2)], [PS(b)])
                        act(cq_f[:, c, 0:n], ps[b][:, 0:n], AF.Copy, [PS(b)], [("cq_f", c)])
                        act(cq_sq[:, c, 0:n], ps[b][:, 0:n], AF.Square, [PS(b)], [("cq_sq", c)])
                    b = next_bank()
                    for c in range(4):
                        mm(ps[b][:, 0:n], ones_bf[:, :], cq_sq[:, c, 0:n], c == 0, c == 3,
                           ["ones_bf", ("cq_sq", c)], [PS(b)])
                    rstd_from(b, n, 512.0, rs2[:, 0:n], "rs2")
                    for c in range(4):
                        stt("dve", cqn[:, c, 0:n], cq_f[:, c, 0:n], gains_s[:, 64 + c:65 + c], rs2[:, 0:n],
                            ALU.mult, ALU.mult, [("cq_f", c), "gains", "rs2"], ["cqn"])
                    for m in range(8):
                        b = next_bank()
                        for kc in range(16):
                            mm(ps[b][:, 0:n], wq_s[:, kc, 512 + m * 128:512 + (m + 1) * 128], nTq2[gi % 2][:, kc, 0:n],
                               kc == 0, kc == 15, [("wq", 1 + m // 4), ("nTq", gi % 2)], [PS(b)])
                        evac(qd[:, m, c0:c0 + n], ps[b][:, 0:n], [PS(b)], ["qd"])

                def q_body_b(gi):
                    c0, n = QG[gi]
                    for h in range(8):
                        b = next_bank()
                        for c in range(4):
                            mm(ps[b][:, 0:n], wuq_s[:, c, h * 256:h * 256 + 128], cqn[:, c, 0:n], c == 0, c == 3,
                               ["wuq", "cqn"], [PS(b)])
                        evac(qn[:, h, c0:c0 + n], ps[b][:, 0:n], [PS(b)], ["qn"])
                    for h in range(8):
                        ba, bb = next_bank(), next_bank()
                        for c in range(4):
                            mm(ps[ba][0:64, 0:n], wuq_s[:, c, h * 256 + 128:h * 256 + 192], cqn[:, c, 0:n],
                               c == 0, c == 3, ["wuq", "cqn"], [PS(ba)])
                        for c in range(4):
                            mm(ps[bb][0:64, 0:n], wuq_s[:, c, h * 256 + 192:h * 256 + 256], cqn[:, c, 0:n],
                               c == 0, c == 3, ["wuq", "cqn"], [PS(bb)])
                        tt("dve", r1[:, 0:n], ps[ba][0:64, 0:n], cq_s[gi % 2][:, 0:n], ALU.mult,
                           [PS(ba), ("cosq", gi % 2)], ["r1"])
                        tt("dve", r2[:, 0:n], ps[bb][0:64, 0:n], sq_s[gi % 2][:, 0:n], ALU.mult,
                           [PS(bb), ("sinq", gi % 2)], ["r2"])
                        tt("dve", qr[0:64, h, c0:c0 + n], r1[:, 0:n], r2[:, 0:n], ALU.add, ["r1", "r2"], ["qr"])

                x_load(0)
                cs_load(0)
                q_sq(0)
                q_pro(0)
                x_load(1)
                cs_load(1)
                for gi in range(4):
                    if gi + 1 < 4:
                        q_sq(gi + 1)
                        q_pro(gi + 1)
                        if gi + 2 < 4:
                            x_load(gi + 2)
                    q_body_a(gi)
                    q_body_b(gi)
                    if gi + 2 < 4:
                        cs_load(gi + 2)
                S.barrier()

            oT_ctx = ExitStack()
            oT = sbt(oT_ctx, "oT", [128, 16, NQ], BF16)

            with ExitStack() as es:
                kbuf = [sbt(es, "kbuf%d" % i, [128, 2, TKV], BF16) for i in range(2)]
                vbuf = [sbt(es, "vbuf%d" % i, [128, 33, 256], BF16) for i in range(2)]
                krT = sbt(es, "krT", [128, TKV], BF16)
                pT = [sbt(es, "pT%d" % i, [128, 390], BF16) for i in range(5)]
                rec = sbt(es, "rec", [128, 390], F32)
                A_s = sbt(es, "A_s", [128, 2, 390], F32)
                tmpd = sbt(es, "tmpd", [128, 390], F32)
                Oc = sbt(es, "Oc", [128, 2, 390], F32)
                sqd = sbt(es, "sqd", [128, 2, 390], BF16)
                rsd = sbt(es, "rsd", [128, 390], F32)
                dma("sp", krT[0:64, :], kr_s.ap(), ["kr_s"], ["krT"])
                memset("pool", krT[64:128, :], 0.0, ["krT_pad"])
                mla_scale = 1.0 / math.sqrt(192.0)
                diff_scale = 1.0 / math.sqrt(128.0)

                def load_unit(u):
                    sl = u % 2
                    if u < 8:
                        dma("sp", kbuf[sl][:, 0, :], kn_s.ap()[u], ["kn_s"], [("kbuf", sl)])
                        dma("sp", vbuf[sl][:, 0:32, 0:128],
                            vm_s.ap()[0:4096, u * 128:(u + 1) * 128].rearrange("(b p) f -> p b f", p=128),
                            ["vm_s"], [("vbuf", sl)])
                        dma("sp", vbuf[sl][0:16, 32, 0:128], vm_s.ap()[4096:4112, u * 128:(u + 1) * 128],
                            ["vm_s"], [("vbuf", sl)])
                    else:
                        hd = u - 8
                        dma("sp", kbuf[sl][:, :, :], kd_s.ap()[2 * hd:2 * hd + 2].rearrange("m p t -> p m t"),
                            ["kd_s"], [("kbuf", sl)])
                        dma("sp", vbuf[sl][:, 0:32, :],
                            vd_s.ap()[0:4096, hd * 256:(hd + 1) * 256].rearrange("(b p) f -> p b f", p=128),
                            ["vd_s"], [("vbuf", sl)])
                        dma("sp", vbuf[sl][0:16, 32, :], vd_s.ap()[4096:4112, hd * 256:(hd + 1) * 256],
                            ["vd_s"], [("vbuf", sl)])

                srot = {"n": 0}
                orot = {"n": 0}
                prot = {"n": 0}
                SKEW = 2

                jobs = []
                for u in range(12):
                    is_mla = u < 8
                    hd = u - 8
                    for (r0, r1e) in GB:
                        for cmap in ([0] if is_mla else [0, 1]):
                            oset = (4 + 2 * (orot["n"] % 2)) if is_mla else 5
                            orot["n"] += 1
                            jobs.append(dict(u=u, sl=u % 2, is_mla=is_mla, hd=hd, r0=r0, r1e=r1e, c0=QB * r0,
                                             width=QB * (r1e - r0), cmap=cmap,
                                             mi=(8 if is_mla else 2 * hd + cmap),
                                             bO=[oset, oset + 1][:(1 if is_mla else 2)], bS=(oset + 1 if is_mla else oset + 2),
                                             nsb=(4 if is_mla else 5), skew=(3 if is_mla else 4),
                                             kblocks=["meta"] + list(range(0, 4 * (r1e - 1) + 4))))

                def geom(J, kb):
                    if kb == "meta":
                        kk, k0, vblk, ra = 16, 4096, 32, J["r0"]
                    else:
                        kk, k0, vblk, ra = 128, 128 * kb, kb, max(J["r0"], kb // 4)
                    a0 = QB * (ra - J["r0"])
                    return kk, k0, vblk, ra, a0, J["width"] - a0, J["c0"] + a0

                def stage_A(J, ki):
                    kb = J["kblocks"][ki]
                    kk, k0, vblk, ra, a0, N, q0 = geom(J, kb)
                    u, sl, hd, cmap, mi = J["u"], J["sl"], J["hd"], J["cmap"], J["mi"]
                    sb_ = srot["n"] % J["nsb"]
                    srot["n"] += 1
                    pS = ps[sb_]
                    if J["is_mla"]:
                        mm(pS[0:kk, 0:N], kbuf[sl][:, 0, k0:k0 + kk], qn[:, u, q0:q0 + N], True, False,
                           [("kbuf", sl), "qn"], [PS(sb_)])
                        mm(pS[0:kk, 0:N], krT[:, k0:k0 + kk], qr[:, u, q0:q0 + N], False, True,
                           ["krT", "krT_pad", "qr", "qr_pad"], [PS(sb_)])
                    else:
                        mm(pS[0:kk, 0:N], kbuf[sl][:, cmap, k0:k0 + kk], qd[:, 2 * hd + cmap, q0:q0 + N],
                           True, True, [("kbuf", sl), "qd"], [PS(sb_)])
                    for r in range(ra, J["r1e"]):
                        cc = QB * (r - ra)
                        if kb == "meta":
                            if r == 0:
                                tt("dve", pS[0:16, cc:cc + QB], pS[0:16, cc:cc + QB],
                                   btile[0:16, mi * 6 + 5, :], ALU.add, [PS(sb_), "btile"], [PS(sb_)])
                        else:
                            jj = kb - 4 * r
                            if -1 <= jj <= 3:
                                tt("dve", pS[:, cc:cc + QB], pS[:, cc:cc + QB],
                                   btile[:, mi * 6 + jj + 1, :], ALU.add, [PS(sb_), "btile"], [PS(sb_)])
                    pi = prot["n"] % 5
                    prot["n"] += 1
                    if J["is_mla"]:
                        act(pT[pi][0:kk, 0:N], pS[0:kk, 0:N], AF.Exp, [PS(sb_)], [("pT", pi)],
                            bias=0.0, scale=mla_scale)
                    else:
                        act(pT[pi][0:kk, 0:N], pS[0:kk, 0:N], AF.Exp, [PS(sb_), "cm"], [("pT", pi)],
                            bias=cm_s[0:kk, mi:mi + 1], scale=diff_scale)
                    return pi

                def stage_B(J, ki, pi):
                    kb = J["kblocks"][ki]
                    kk, k0, vblk, ra, a0, N, q0 = geom(J, kb)
                    sl = J["sl"]
                    first = ki == 0
                    last = ki == len(J["kblocks"]) - 1
                    for d, bo in enumerate(J["bO"]):
                        mm(ps[bo][:, a0:a0 + N], vbuf[sl][0:kk, vblk, d * 128:(d + 1) * 128],
                           pT[pi][0:kk, 0:N], first, last, [("vbuf", sl), ("pT", pi)], [PS(bo)])
                    mm(ps[J["bS"]][:, a0:a0 + N], ones_bf[0:kk, :], pT[pi][0:kk, 0:N], first, last,
                       ["ones_bf", ("pT", pi)], [PS(J["bS"])])

                later = []

                def stage_F(J):
                    u, hd, cmap, c0, width, bO, bS = J["u"], J["hd"], J["cmap"], J["c0"], J["width"], J["bO"], J["bS"]
                    if not J["is_mla"]:
                        for d in range(2):
                            vcopy(Oc[:, d, 0:width], ps[bO[d]][:, 0:width], [PS(bO[d])], [("Oc", d)])
                    act(rec[:, 0:width], ps[bS][:, 0:width], AF.Ln, [PS(bS)], ["rec"])
                    act(rec[:, 0:width], rec[:, 0:width], AF.Exp, ["rec"], ["rec"], bias=0.0, scale=-1.0)
                    if J["is_mla"]:
                        tt("dve", oT[:, u, c0:c0 + width], ps[bO[0]][:, 0:width], rec[:, 0:width], ALU.mult,
                           [PS(bO[0]), "rec"], ["oT"])
                    elif cmap == 0:
                        for d in range(2):
                            tt("dve", A_s[:, d, 0:width], Oc[:, d, 0:width], rec[:, 0:width], ALU.mult,
                               [("Oc", d), "rec"], [("A", d)])
                    else:
                        for d in range(2):
                            tt("dve", tmpd[:, 0:width], Oc[:, d, 0:width], rec[:, 0:width], ALU.mult,
                               [("Oc", d), "rec"], ["tmpd"])
                            stt("dve", A_s[:, d, 0:width], tmpd[:, 0:width], lam_s[:, 1:2], A_s[:, d, 0:width],
                                ALU.mult, ALU.add, ["tmpd", "lam", ("A", d)], [("A", d)])
                            act(sqd[:, d, 0:width], A_s[:, d, 0:width], AF.Square, [("A", d)], [("sqd", d)])

                        def _f2(hd=hd, c0=c0, width=width):
                            sb_ = srot["n"] % 5
                            srot["n"] += 1
                            for d in range(2):
                                mm(ps[sb_][:, 0:width], ones_bf[:, :], sqd[:, d, 0:width], d == 0, d == 1,
                                   ["ones_bf", ("sqd", d)], [PS(sb_)])
                            rstd_from(sb_, width, 256.0, rsd[:, 0:width], "rsd")
                            for d in range(2):
                                stt("dve", oT[:, 8 + 2 * hd + d, c0:c0 + width], A_s[:, d, 0:width],
                                    gsub_s[:, d:d + 1], rsd[:, 0:width], ALU.mult, ALU.mult,
                                    [("A", d), "gsub", "rsd"], ["oT"])
                        later.append([3, _f2])

                def tick():
                    for it in list(later):
                        it[0] -= 1
                        if it[0] <= 0:
                            later.remove(it)
                            it[1]()

                load_unit(0)
                load_unit(1)
                b_started = set()
                pend = []

                def do_B():
                    Jb, kib, pib = pend.pop(0)
                    ub = Jb["u"]
                    if ub not in b_started:
                        b_started.add(ub)
                        if ub >= 1 and ub + 1 < 12:
                            load_unit(ub + 1)
                    stage_B(Jb, kib, pib)
                    if kib == len(Jb["kblocks"]) - 1:
                        stage_F(Jb)

                for J in jobs:
                    for ki in range(len(J["kblocks"])):
                        pi = stage_A(J, ki)
                        pend.append((J, ki, pi))
                        while len(pend) > J["skew"]:
                            do_B()
                        tick()
                while pend:
                    do_B()
                    tick()
                for _ in range(4):
                    tick()
                S.barrier()
        right_ctx.close()
        n2_ctx = ExitStack()
        n2T = sbt(n2_ctx, "n2T", [128, 16, NQ], BF16, side="right")
        wup_b = [sbt(n2_ctx, "wup_b%d" % i, [128, 16, 256], BF16, side="right") for i in range(3)]

        def load_wup(f):
            dma("pool", wup_b[f % 3][:, :, :], wup.ap()[f].rearrange("p (k n) -> p k n", n=256), (),
                [("wup_b", f % 3)])
        with ExitStack() as es:
            aT = sbt(es, "aT", [128, 16, NQ], F32)
            wo_b = [sbt(es, "wo_b%d" % i, [128, 2, 2048], BF16) for i in range(2)]
            sqb = [sbt(es, "sqb%d" % i, [128, 390], BF16) for i in range(3)]
            rs1 = sbt(es, "rs1", [128, NQ], F32)
            rs2 = rs1
            xqc = [sbt(es, "xqc%d" % i, [128, NQ], F32) for i in range(4)]

            def load_xq(ch):
                dma("sp", xqc[ch % 4][:], xq.ap()[:, ch, :], (), [("xqc", ch % 4)])

            def load_wo(g):
                dma("pool", wo_b[g % 2][:, :, :], wo.ap()[2 * g:2 * g + 2].rearrange("o p f -> p o f"), (),
                    [("wo_b", g % 2)])

            load_wo(0)
            load_wo(1)
            for ch in range(4):
                load_xq(ch)
            pending = []
            sqi = 0
            for oc in range(16):
                g = oc // 2
                if oc % 2 == 0 and 2 <= g + 1 < 8:
                    load_wo(g + 1)
                for gi, (c0, n) in enumerate(CG):
                    b = next_bank(0, 5)
                    for kc in range(16):
                        mm(ps[b][:, 0:n], wo_b[g % 2][:, oc % 2, kc * 128:(kc + 1) * 128], oT[:, kc, c0:c0 + n],
                           kc == 0, kc == 15, [("wo_b", g % 2), "oT"], [PS(b)])
                    for f in pending:
                        f()
                    pending = []
                    si = sqi % 3
                    sqi += 1
                    act(aT[:, oc, c0:c0 + n], ps[b][:, 0:n], AF.Copy, [PS(b)], [("aT", oc)])
                    act(sqb[si][:, 0:n], ps[b][:, 0:n], AF.Square, [PS(b)], [("sqb", si)])

                    def _f(si=si, gi=gi, n=n, oc=oc):
                        mm(ps[5 + gi][:, 0:n], ones_bf[:, :], sqb[si][:, 0:n], oc == 0, oc == 15,
                           ["ones_bf", ("sqb", si)], [PS(5 + gi)])
                    pending.append(_f)
            for f in pending:
                f()
            pending = []
            load_wup(0)
            load_wup(1)
            for gi, (c0, n) in enumerate(CG):
                rstd_from(5 + gi, n, 2048.0, rs1[:, c0:c0 + n], ("rs1", gi))
            RS1 = [("rs1", gi) for gi in range(3)]
            sqi = 0
            for ch in range(16):
                xs = ch % 4
                stt("dve", aT[:, ch, :], aT[:, ch, :], gains_s[:, 16 + ch:17 + ch], rs1[:, :], ALU.mult, ALU.mult,
                    [("aT", ch), "gains"] + RS1, [("aT", ch)])
                tt("dve", aT[:, ch, :], aT[:, ch, :], xqc[xs][:], ALU.add,
                   [("aT", ch), ("xqc", xs)], [("aT", ch)])
                if ch + 4 < 16:
                    load_xq(ch + 4)
                for gi, (c0, n) in enumerate(CG):
                    si = sqi % 3
                    sqi += 1
                    act(sqb[si][:, 0:n], aT[:, ch, c0:c0 + n], AF.Square, [("aT", ch)], [("sqb", si)])
                    mm(ps[5 + gi][:, 0:n], ones_bf[:, :], sqb[si][:, 0:n], ch == 0, ch == 15,
                       ["ones_bf", ("sqb", si)], [PS(5 + gi)])
                dma("act", h1_s.ap()[:, ch, :].rearrange("p (r t) -> p r t", t=128),
                    aT[:, ch, :].rearrange("p (r c) -> p r c", c=QB)[:, :, 2:QB], [("aT", ch)], ["h1_s"])
            for gi, (c0, n) in enumerate(CG):
                rstd_from(5 + gi, n, 2048.0, rs2[:, c0:c0 + n], ("rs1", gi))
            RS2 = [("rs1", gi) for gi in range(3)]
            for ch in range(16):
                stt("dve", n2T[:, ch, :], aT[:, ch, :], gains_s[:, 32 + ch:33 + ch],
                    rs2[:, :], ALU.mult, ALU.mult, [("aT", ch), "gains"] + RS2, [("n2T", ch)])
            S.barrier()
        oT_ctx.close()

        act_ctx = ExitStack()
        actT = sbt(act_ctx, "actT", [128, 44, 1024], BF16)
        wdn_b = [sbt(act_ctx, "wdn_b%d" % i, [128, 44, 128], BF16) for i in range(2)]

        def load_wdn(oc):
            dma("pool", wdn_b[oc % 2][:, :, :], wdn.ap()[oc].rearrange("p (k n) -> p k n", n=128), (),
                [("wdn_b", oc % 2)])
        with ExitStack() as es:
            tg = [sbt(es, "tg%d" % i, [128, 3, 128], F32) for i in range(2)]
            tv = [sbt(es, "tv%d" % i, [128, 3, 128], F32) for i in range(2)]
            sg = [sbt(es, "sg%d" % i, [128, 3, 128], F32) for i in range(2)]

            ti = 0
            for f in range(44):
                if f + 2 < 44:
                    load_wup(f + 2)
                if f == 36:
                    load_wdn(0)
                    load_wdn(1)
                for gi, (c0, n) in enumerate(CG):
                    nb = n // QB
                    r0 = GB[gi][0]
                    sl = ti % 2
                    ti += 1
                    tgs = [tg[sl], tv[sl]]
                    for part in range(2):
                        b = next_bank()
                        for kc in range(16):
                            mm(ps[b][:, 0:n], wup_b[f % 3][:, kc, part * 128:(part + 1) * 128], n2T[:, kc, c0:c0 + n],
                               kc == 0, kc == 15, [("wup_b", f % 3), "n2T"], [PS(b)])
                        pv = ps[b][:, 0:n].rearrange("p (r c) -> p r c", c=QB)
                        cb = f * 8 + part * 4
                        tk = ("t", part, sl)
                        act(tgs[part][:, 0:nb, :], pv[:, :, 2:QB], AF.Identity, [PS(b), "convp"], [tk],
                            bias=convp_s[:, cb + 3:cb + 4], scale=convp_s[:, cb + 2:cb + 3])
                        stt("dve", tgs[part][:, 0:nb, :], pv[:, :, 1:QB - 1], convp_s[:, cb + 1:cb + 2],
                            tgs[part][:, 0:nb, :], ALU.mult, ALU.add, [PS(b), "convp", tk], [tk])
                        stt("dve", tgs[part][:, 0:nb, :], pv[:, :, 0:QB - 2], convp_s[:, cb:cb + 1],
                            tgs[part][:, 0:nb, :], ALU.mult, ALU.add, [PS(b), "convp", tk], [tk])
                    act(sg[sl][:, 0:nb, :], tg[sl][:, 0:nb, :], AF.Silu, [("t", 0, sl)], [("sg", sl)])
                    tt("pool", actT[:, f, r0 * 128:(r0 + nb) * 128].rearrange("p (r t) -> p r t", t=128),
                       sg[sl][:, 0:nb, :], tv[sl][:, 0:nb, :], ALU.mult, [("sg", sl), ("t", 1, sl)], [("actT", f)])
            S.barrier()
        n2_ctx.close()

        with ExitStack() as es:
            fT = sbt(es, "fT", [128, 16, 1024], F32)
            sqb = [sbt(es, "sqc%d" % i, [128, 512], BF16) for i in range(3)]
            rs3 = sbt(es, "rs3", [128, 1024], F32)
            h1c = [sbt(es, "h1c%d" % i, [128, 1024], F32) for i in range(3)]
            ACTK = [("actT", f) for f in range(44)]

            pending = []
            sqi = 0
            for oc in range(16):
                if 2 <= oc + 1 < 16:
                    load_wdn(oc + 1)
                for tgi in range(2):
                    b = next_bank(0, 6)
                    for kc in range(44):
                        mm(ps[b][:, 0:512], wdn_b[oc % 2][:, kc, :], actT[:, kc, tgi * 512:(tgi + 1) * 512],
                           kc == 0, kc == 43, [("wdn_b", oc % 2)] + (ACTK if kc == 0 else []), [PS(b)])
                    for f_ in pending:
                        f_()
                    pending = []
                    si = sqi % 3
                    sqi += 1
                    act(fT[:, oc, tgi * 512:(tgi + 1) * 512], ps[b][:, 0:512], AF.Copy, [PS(b)], [("fT", oc)])
                    act(sqb[si][:, :], ps[b][:, 0:512], AF.Square, [PS(b)], [("sqb", si)])

                    def _f(si=si, tgi=tgi, oc=oc):
                        mm(ps[6 + tgi][:, 0:512], ones_bf[:, :], sqb[si][:, :], oc == 0, oc == 15,
                           ["ones_bf", ("sqb", si)], [PS(6 + tgi)])
                    pending.append(_f)
            for f_ in pending:
                f_()
            for tgi in range(2):
                rstd_from(6 + tgi, 512, 2048.0, rs3[:, tgi * 512:(tgi + 1) * 512], ("rs3", tgi))
            RS3 = [("rs3", 0), ("rs3", 1)]
            outs = []
            for ch in range(16):
                xs = ch % 3
                dma("sp", h1c[xs][:], h1_s.ap()[:, ch, :], ["h1_s"], [("h1c", xs)])
                stt("dve", fT[:, ch, :], fT[:, ch, :], gains_s[:, 48 + ch:49 + ch], rs3[:, :], ALU.mult, ALU.mult,
                    [("fT", ch), "gains"] + RS3, [("fT", ch)])
                tt("dve", fT[:, ch, :], fT[:, ch, :], h1c[xs][:], ALU.add,
                   [("fT", ch), ("h1c", xs)], [("fT", ch)])
                dma("act", out.ap()[:, ch, :], fT[:, ch, :], [("fT", ch)], [("out", ch)])
                outs.append(("out", ch))
            S.add("sp", None, reads=outs)
        act_ctx.close()

        S.finalize()
        sems = {s: top.enter_context(nc.semaphore("s_" + s)) for s in S.streams}
        block = top.enter_context(nc.Block())
        block.tensor(lambda e: S.emit("pe", e, sems))
        block.scalar(lambda e: S.emit("act", e, sems))
        block.vector(lambda e: S.emit("dve", e, sems))
        block.gpsimd(lambda e: S.emit("pool", e, sems))
        block.sync(lambda e: S.emit("sp", e, sems))
    return nc


def _t5_bucket(n):
    n = np.maximum(n, 0)
    nf = np.maximum(n, 1).astype(np.float32)
    large = 16 + (np.log(nf / np.float32(16.0)) / np.float32(math.log(8.0)) * np.float32(16.0)).astype(np.int32)
    large = np.minimum(large, 31)
    return np.where(n < 16, n, large)


def _fm(w, ncols=None):
    K, N = w.shape
    return np.ascontiguousarray(w.reshape(K // 128, 128, N).transpose(1, 0, 2))


def _prep_shared(inp):
    f = np.float32
    w_in = inp["w_in"][0]
    cq, ckv, kr, qdw, kdw, vdw = np.split(w_in, np.cumsum([512, 256, 64, 1024, 1024])[:], axis=1)
    kr_sw = np.concatenate([kr[:, 32:64], kr[:, 0:32]], axis=1)
    sh = {}
    sh["wkv"] = _fm(np.concatenate([ckv, kr, kr_sw, kdw, vdw], axis=1))
    sh["wq"] = _fm(np.concatenate([cq, qdw], axis=1))
    w_ukv = inp["w_ukv"][0].reshape(256, 8, 256)
    sh["wukv"] = _fm(np.concatenate([w_ukv[:, :, 0:128].reshape(256, 1024), w_ukv[:, :, 128:256].reshape(256, 1024)], axis=1))
    w_uq = inp["w_uq"][0].reshape(512, 8, 192)
    wuq = np.concatenate([w_uq[:, :, 0:128], w_uq[:, :, 128:192], w_uq[:, :, 160:192], w_uq[:, :, 128:160]], axis=2)
    sh["wuq"] = _fm(wuq.reshape(512, 2048))
    w_o = inp["w_o"][0]
    sh["wo"] = np.ascontiguousarray(w_o.reshape(16, 128, 16, 128).transpose(2, 1, 0, 3).reshape(16, 128, 2048))
    w_up = inp["w_up"][0]
    g = w_up[:, :5632].reshape(16, 128, 44, 128)
    v = w_up[:, 5632:].reshape(16, 128, 44, 128)
    gv = np.concatenate([g, v], axis=3)
    sh["wup"] = np.ascontiguousarray(gv.transpose(2, 1, 0, 3).reshape(44, 128, 4096))
    w_dn = inp["w_down"][0]
    sh["wdn"] = np.ascontiguousarray(w_dn.reshape(44, 128, 16, 128).transpose(2, 1, 0, 3).reshape(16, 128, 5632))
    cols = []
    for k in ("g_attn_pre", "g_attn_post", "g_ffn_pre", "g_ffn_post"):
        cols.append(inp[k][0].reshape(16, 128).T)
    cols.append(inp["g_cq"][0].reshape(4, 128).T)
    cols.append(inp["g_ckv"][0].reshape(2, 128).T)
    cols.append(inp["g_diff_sub"][0].reshape(2, 128).T)
    sh["gains"] = np.ascontiguousarray(np.concatenate(cols, axis=1).astype(f))
    cw = inp["conv_w"][0]
    cb = inp["conv_b"][0]
    cp = np.zeros((128, 44, 2, 4), f)
    for part in range(2):
        o = part * 5632
        for j in range(3):
            cp[:, :, part, j] = cw[j, o:o + 5632].reshape(44, 128).T
        cp[:, :, part, 3] = cb[o:o + 5632].reshape(44, 128).T
    sh["convp"] = np.ascontiguousarray(cp.reshape(128, 352))
    sh["rb"] = np.ascontiguousarray(inp["rel_bias"].astype(f))
    sh["rb31"] = np.ascontiguousarray(np.repeat(inp["rel_bias"][31:32, :], 128, axis=0).astype(f))
    sh["lamv"] = np.ascontiguousarray(np.stack([inp["lambda_q1"][0], inp["lambda_q2"][0],
                                                inp["lambda_k1"][0], inp["lambda_k2"][0]], axis=1).astype(f))
    inv = 10000.0 ** (-np.arange(32, dtype=np.float64) / 32.0)
    posk = np.concatenate([16 + np.arange(4096), np.arange(16)]).astype(np.float64)
    ang = posk[None, :] * inv[:, None]
    sh["cosk"] = np.ascontiguousarray(np.concatenate([np.cos(ang), np.cos(ang)], axis=0).astype(f))
    sh["sink"] = np.ascontiguousarray(np.concatenate([-np.sin(ang), np.sin(ang)], axis=0).astype(f))
    sh["_inv"] = inv
    return sh


def _prep_core(inp, sh, c, xkv_b):
    f = np.float32
    b, i = c // 4, c % 4
    m = {k: v for k, v in sh.items() if not k.startswith("_")}
    m["xkv"] = xkv_b[b]
    seq = np.concatenate([inp["meta_tokens"].astype(f), inp["x"][b]], axis=0)
    pos = np.concatenate([16 + 128 * (4 * r + i) - 2 + np.arange(QB) for r in range(8)])
    xo = seq[pos]
    m["xq"] = np.ascontiguousarray(xo.T.reshape(16, 128, NQ).transpose(1, 0, 2))
    ang = pos.astype(np.float64)[None, :] * sh["_inv"][:, None]
    m["cosq"] = np.ascontiguousarray(np.concatenate([np.cos(ang), np.cos(ang)], axis=0).astype(f))
    m["sinq"] = np.ascontiguousarray(np.concatenate([-np.sin(ang), np.sin(ang)], axis=0).astype(f))
    u = np.arange(769) - 513
    rrel = u + 128 * i
    ohm = np.zeros((33, 769), f)
    bk = _t5_bucket(rrel)
    for idx in range(769):
        if rrel[idx] < 0:
            ohm[32, idx] = 1.0
        else:
            ohm[bk[idx], idx] = 1.0
    m["oh"] = ohm
    return m


_NC_CACHE = {}


def kernel(**inputs):
    inp = {k: np.asarray(v) for k, v in inputs.items()}
    if "nc" not in _NC_CACHE:
        _NC_CACHE["nc"] = build_nc()
    nc = _NC_CACHE["nc"]
    sh = _prep_shared(inp)
    xkv_b = []
    for b in range(2):
        seqk = np.concatenate([inp["x"][b], inp["meta_tokens"].astype(np.float32)], axis=0)
        xkv_b.append(np.ascontiguousarray(seqk.T.reshape(16, 128, TKV).transpose(1, 0, 2)))
    in_maps = [_prep_core(inp, sh, c, xkv_b) for c in range(NCORES)]
    res = run_bass_kernel_spmd(nc, in_maps, core_ids=list(range(NCORES)))
    outp = np.zeros((2, 4096, 2048), np.float32)
    for c in range(NCORES):
        b, i = c // 4, c % 4
        o = np.asarray(res.results[c]["out"]).reshape(128, 16, 8, 128)
        for r in range(8):
            g = 4 * r + i
            outp[b, 128 * g:128 * (g + 1), :] = o[:, :, r, :].transpose(2, 1, 0).reshape(128, 2048)
    return outp
```

```python
import math
from contextlib import ExitStack

import numpy as np
import concourse.bass as bass
import concourse.mybir as mybir
from concourse.bass_utils import run_bass_kernel_spmd

F32 = mybir.dt.float32
BF16 = mybir.dt.bfloat16
ALU = mybir.AluOpType
AF = mybir.ActivationFunctionType

NCORES = 8
TKV = 4112
NQ = 1040
QB = 130
EPS = 1e-6
NEG = -30000.0
CG = [(0, 390), (390, 390), (780, 260)]
GB = [(0, 3), (3, 6), (6, 8)]


class _Op:
    __slots__ = ("stream", "eng", "fn", "deps", "signal", "count")


class Sched:
    ENGS = ("pe", "act", "dve", "pool", "sp")

    NDMASEM = 16

    def __init__(self, same_engine_sync=True):
        self.q = {e: [] for e in self.ENGS}
        self.lastw = {}
        self.readers = {}
        self.last_real = {}
        self.same_engine_sync = same_engine_sync
        self.dma_rr = {}

    def add(self, eng, fn, reads=(), writes=(), dma=False):
        op = _Op()
        op.eng = eng
        if dma:
            k = self.dma_rr.get(eng, 0)
            self.dma_rr[eng] = k + 1
            op.stream = "dma_%s_%d" % (eng, k % self.NDMASEM)
        else:
            op.stream = eng
        op.fn = fn
        op.signal = bool(dma)
        op.count = 0
        deps = []
        seen = set()

        def _add(d):
            if d is None or id(d) in seen:
                return
            seen.add(id(d))
            if d.stream == op.stream and not dma:
                if op.stream == "pe" or not self.same_engine_sync:
                    return
            deps.append(d)

        if dma:
            _add(self.last_real.get(op.stream))
        for k in reads:
            _add(self.lastw.get(k))
        for k in writes:
            _add(self.lastw.get(k))
            for r in self.readers.get(k, {}).values():
                _add(r)
        op.deps = deps
        for d in deps:
            d.signal = True
        for k in reads:
            self.readers.setdefault(k, {})[op.stream] = op
        for k in writes:
            self.lastw[k] = op
            self.readers[k] = {}
        self.q[eng].append(op)
        self.last_real[op.stream] = op
        return op

    def barrier(self):
        lasts = list(self.last_real.values())
        for e in self.ENGS:
            op = _Op()
            op.eng = e
            op.stream = e
            op.fn = None
            op.signal = False
            op.count = 0
            op.deps = list(lasts)
            for d in lasts:
                d.signal = True
            self.q[e].append(op)
        self.lastw = {}
        self.readers = {}

    def finalize(self):
        cnt = {}
        for e in self.ENGS:
            for op in self.q[e]:
                if op.signal:
                    cnt[op.stream] = cnt.get(op.stream, 0) + 1
                    op.count = cnt[op.stream]
        for s, c in cnt.items():
            assert c * (16 if s.startswith("dma_") else 1) < 60000, (s, c)
        self.totals = cnt
        self.streams = sorted(set(list(cnt.keys()) + list(self.ENGS)))

    def emit(self, eng_name, engine, sems):
        waited = {}
        for op in self.q[eng_name]:
            for d in op.deps:
                val = d.count * (16 if d.stream.startswith("dma_") else 1)
                if waited.get(d.stream, 0) < val:
                    engine.wait_ge(sems[d.stream], val)
                    waited[d.stream] = val
            if op.fn is None:
                continue
            ins = op.fn(engine)
            if op.signal:
                ins.then_inc(sems[op.stream], 16 if op.stream.startswith("dma_") else 1)


def build_nc():
    nc = bass.Bass("TRN2", target_bir_lowering=False)
    S = Sched()

    def din(name, shape, dt=F32):
        return nc.dram_tensor(name, shape, dt, kind="ExternalInput")

    xkv = din("xkv", [128, 16, TKV])
    xq = din("xq", [128, 16, NQ])
    wkv = din("wkv", [128, 16, 2432])
    wukv = din("wukv", [128, 2, 2048])
    wq = din("wq", [128, 16, 1536])
    wuq = din("wuq", [128, 4, 2048])
    wo = din("wo", [16, 128, 2048])
    wup = din("wup", [44, 128, 4096])
    wdn = din("wdn", [16, 128, 5632])
    gains = din("gains", [128, 72])
    convp = din("convp", [128, 352])
    rb = din("rb", [32, 8])
    rb31 = din("rb31", [128, 8])
    lamv = din("lamv", [128, 4])
    oh = din("oh", [33, 769])
    cosq = din("cosq", [64, NQ])
    sinq = din("sinq", [64, NQ])
    cosk = din("cosk", [64, TKV])
    sink = din("sink", [64, TKV])
    out = nc.dram_tensor("out", [128, 16, 1024], F32, kind="ExternalOutput")

    kn_s = nc.dram_tensor("kn_s", [8, 128, TKV], BF16)
    kd_s = nc.dram_tensor("kd_s", [8, 128, TKV], BF16)
    kr_s = nc.dram_tensor("kr_s", [64, TKV], BF16)
    vm_s = nc.dram_tensor("vm_s", [TKV, 1024], BF16)
    vd_s = nc.dram_tensor("vd_s", [TKV, 1024], BF16)
    tscr = nc.dram_tensor("tscr", [9, 128, 769], F32)
    h1_s = nc.dram_tensor("h1_s", [128, 16, 1024], F32)

    def mm(o, lhsT, rhs, start, stop, reads, writes):
        S.add("pe", lambda e: e.matmul(o, lhsT=lhsT, rhs=rhs, start=start, stop=stop), reads, writes)

    def act(o, i, func, reads, writes, bias=0.0, scale=1.0):
        S.add("act", lambda e: e.activation(out=o, in_=i, func=func, bias=bias, scale=scale), reads, writes)

    def tt(eng, o, a, b, op, reads, writes):
        S.add(eng, lambda e: e.tensor_tensor(out=o, in0=a, in1=b, op=op), reads, writes)

    def ts(eng, o, a, s1, s2, op0, op1, reads, writes):
        if op1 is None:
            S.add(eng, lambda e: e.tensor_scalar(out=o, in0=a, scalar1=s1, scalar2=None, op0=op0), reads, writes)
        else:
            S.add(eng, lambda e: e.tensor_scalar(out=o, in0=a, scalar1=s1, scalar2=s2, op0=op0, op1=op1), reads, writes)

    def stt(eng, o, a, sc, b, op0, op1, reads, writes):
        S.add(eng, lambda e: e.scalar_tensor_tensor(out=o, in0=a, scalar=sc, in1=b, op0=op0, op1=op1), reads, writes)

    def recip(o, i, reads, writes):
        S.add("dve", lambda e: e.reciprocal(out=o, in_=i), reads, writes)

    def vcopy(o, i, reads, writes):
        S.add("dve", lambda e: e.tensor_copy(out=o, in_=i), reads, writes)

    def memset(eng, o, v, writes):
        S.add(eng, lambda e: e.memset(o, v), (), writes)

    def dma(eng, o, i, reads, writes):
        S.add(eng, lambda e: e.dma_start(out=o, in_=i), reads, writes, dma=True)

    evac_flip = [0]

    def evac(o, i, reads, writes):
        evac_flip[0] ^= 1
        if evac_flip[0]:
            act(o, i, AF.Copy, reads, writes)
        else:
            vcopy(o, i, reads, writes)

    with ExitStack() as top:
        def sbt(es, name, shape, dt, side=None):
            if side is None:
                return es.enter_context(nc.sbuf_tensor(name, shape, dt))
            return es.enter_context(nc.sbuf_tensor(name, shape, dt, side=side))

        ps = [top.enter_context(nc.psum_tensor("ps%d" % b, [128, 512], F32)) for b in range(8)]

        rot = {"n": 0}

        def next_bank(lo=0, hi=8):
            b = lo + rot["n"] % (hi - lo)
            rot["n"] += 1
            return b

        def PS(b):
            return ("ps", b)

        gains_s = sbt(top, "gains_s", [128, 72], F32)
        convp_s = sbt(top, "convp_s", [128, 352], F32)
        cm_s = sbt(top, "cm_s", [128, 8], F32)
        ones_bf = sbt(top, "ones_bf", [128, 128], BF16)
        ones_f = sbt(top, "ones_f", [128, 128], F32)
        eps_s = sbt(top, "eps_s", [128, 1], F32)
        lam_s = sbt(top, "lam_s", [128, 4], F32)
        gsub_s = sbt(top, "gsub_s", [128, 2], F32)
        right_ctx = ExitStack()
        btile = sbt(right_ctx, "btile", [128, 9 * 6, QB], F32, side="right")

        dma("sp", gains_s[:], gains.ap(), (), ["gains"])
        dma("sp", convp_s[:], convp.ap(), (), ["convp"])
        dma("sp", cm_s[:], rb31.ap(), (), ["cm"])
        memset("dve", ones_bf[:], 1.0, ["ones_bf"])
        memset("dve", ones_f[:], 1.0, ["ones_f"])
        memset("dve", eps_s[:], EPS, ["eps"])

        setup_ctx = ExitStack()
        lamv_s = sbt(setup_ctx, "lamv_s", [128, 4], F32, side="right")
        prod_s = sbt(setup_ctx, "prod_s", [128, 2], F32, side="right")
        e_s = sbt(setup_ctx, "e_s", [128, 2], F32, side="right")
        rb_s = sbt(setup_ctx, "rb_s", [32, 8], F32, side="right")
        oh_s = sbt(setup_ctx, "oh_s", [33, 769], F32, side="right")
        lhsb = sbt(setup_ctx, "lhsb", [33, 9, 128], F32, side="right")
        T_s = sbt(setup_ctx, "T_s", [128, 769], F32, side="right")

        def emit_setup():
            dma("sp", lamv_s[:], lamv.ap(), (), ["lamv"])
            dma("sp", rb_s[:], rb.ap(), (), ["rb"])
            dma("sp", oh_s[:], oh.ap(), (), ["oh"])
            tt("dve", prod_s[:], lamv_s[:, 0:2], lamv_s[:, 2:4], ALU.mult, ["lamv"], ["prod"])
            b0 = next_bank()
            mm(ps[b0][:, 0:2], ones_f[:, :], prod_s[:, :], True, True, ["ones_f", "prod"], [PS(b0)])
            act(e_s[:], ps[b0][:, 0:2], AF.Exp, [PS(b0)], ["e_s"])
            tt("dve", lam_s[:, 0:1], e_s[:, 0:1], e_s[:, 1:2], ALU.subtract, ["e_s"], ["lam"])
            ts("dve", lam_s[:, 0:1], lam_s[:, 0:1], 0.2, None, ALU.add, None, ["lam"], ["lam"])
            ts("dve", lam_s[:, 1:2], lam_s[:, 0:1], -1.0, None, ALU.mult, None, ["lam"], ["lam"])
            ts("dve", gsub_s[:], gains_s[:, 70:72], 0.8, None, ALU.mult, None, ["gains"], ["gsub"])
            tt("dve", rb_s[:], rb_s[:], cm_s[0:32, :], ALU.subtract, ["rb", "cm"], ["rb"])
            ts("dve", rb_s[:], rb_s[:], math.sqrt(128.0), None, ALU.mult, None, ["rb"], ["rb"])
            memset("dve", lhsb[:], 1.0, ["lhsb"])
            for m in range(8):
                ts("dve", lhsb[0:32, m, :], lhsb[0:32, m, :], rb_s[:, m:m + 1], None, ALU.mult, None,
                   ["lhsb", "rb"], ["lhsb"])
            memset("dve", lhsb[0:32, 8, :], 0.0, ["lhsb"])
            memset("dve", lhsb[32:33, :, :], NEG, ["lhsb"])

        def emit_setup_map(m):
            ba, bb = next_bank(), next_bank()
            mm(ps[ba][:, 0:512], lhsb[:, m, :], oh_s[:, 0:512], True, True, ["lhsb", "oh"], [PS(ba)])
            mm(ps[bb][:, 0:257], lhsb[:, m, :], oh_s[:, 512:769], True, True, ["lhsb", "oh"], [PS(bb)])
            vcopy(T_s[:, 0:512], ps[ba][:, 0:512], [PS(ba)], ["T_s"])
            vcopy(T_s[:, 512:769], ps[bb][:, 0:257], [PS(bb)], ["T_s"])
            dma("pool", tscr.ap()[m], T_s[:], ["T_s"], [("tscr", m)])
            for jx in range(5):
                jj = jx - 1
                src = bass.AP(tensor=tscr, offset=m * 128 * 769 + 511 - 128 * jj, ap=[[768, 128], [1, QB]])
                dma("pool", btile[:, m * 6 + jx, :], src, [("tscr", m)], ["btile"])
            src = bass.AP(tensor=tscr, offset=m * 128 * 769 + 527, ap=[[768, 16], [1, QB]])
            dma("pool", btile[0:16, m * 6 + 5, :], src, [("tscr", m)], ["btile"])

        def rstd_from(bank, n, nfeat, rs_ap, rs_key):
            act(rs_ap, ps[bank][:, 0:n], AF.Ln, [PS(bank), "eps"], [rs_key], bias=eps_s[:, 0:1], scale=1.0 / nfeat)
            act(rs_ap, rs_ap, AF.Exp, [rs_key], [rs_key], bias=0.0, scale=-0.5)

        with ExitStack() as es:
            wkv_s = sbt(es, "wkv_s", [128, 16, 2432], BF16)
            wukv_s = sbt(es, "wukv_s", [128, 2, 2048], BF16)
            xg = sbt(es, "xg", [128, 16, 256], F32)
            sq = sbt(es, "sq", [128, 16, 256], BF16)
            nT = [sbt(es, "nT%d" % i, [128, 16, 256], BF16) for i in range(2)]
            rs = sbt(es, "rs", [128, 256], F32)
            rs2 = sbt(es, "rs2", [128, 256], F32)
            ck = [sbt(es, "ck%d" % i, [64, 256], F32) for i in range(2)]
            sk = [sbt(es, "sk%d" % i, [64, 256], F32) for i in range(2)]
            r1 = sbt(es, "r1", [64, 256], F32)
            r2 = sbt(es, "r2", [64, 256], F32)
            kr_o = sbt(es, "kr_o", [64, 256], BF16)
            kd_o = sbt(es, "kd_o", [128, 8, 256], BF16)
            kn_o = sbt(es, "kn_o", [128, 8, 256], BF16)
            vd_o = sbt(es, "vd_o", [128, 2, 1024], BF16)
            vm_o = sbt(es, "vm_o", [128, 2, 1024], BF16)
            ckv_f = sbt(es, "ckv_f", [128, 2, 256], F32)
            ckv_sq = sbt(es, "ckv_sq", [128, 2, 256], BF16)
            ckvn = sbt(es, "ckvn", [128, 2, 256], BF16)

            WP = [(0, 384), (384, 896), (896, 1408), (1408, 1920), (1920, 2432)]
            for pi_, (a, b_) in enumerate(WP):
                dma("pool", wkv_s[:, :, a:b_], wkv.ap()[:, :, a:b_], ([("wkv", pi_ - 1)] if pi_ else ()),
                    [("wkv", pi_)])
            dma("pool", wukv_s[:], wukv.ap(), [("wkv", len(WP) - 1)], ["wukv"])

            def WK(col):
                for pi_, (a, b_) in enumerate(WP):
                    if a <= col < b_:
                        return ("wkv", pi_)

            groups = [(t0, 256) for t0 in range(0, 4096, 256)] + [(4096, 16)]

            def load_x(gi):
                t0, n = groups[gi]
                sl = gi % 2
                dma("sp", xg[:, :, 0:n], xkv.ap()[:, :, t0:t0 + n], (), ["xg"])
                dma("sp", ck[sl][:, 0:n], cosk.ap()[:, t0:t0 + n], (), [("ck", sl)])
                dma("sp", sk[sl][:, 0:n], sink.ap()[:, t0:t0 + n], (), [("sk", sl)])

            def pre_square(gi):
                t0, n = groups[gi]
                act(sq[:, :, 0:n], xg[:, :, 0:n], AF.Square, ["xg"], ["sq"])

            def prologue(gi):
                t0, n = groups[gi]
                sl = gi % 2
                nTc = nT[sl]
                b = next_bank()
                for kc in range(16):
                    mm(ps[b][:, 0:n], ones_bf[:, :], sq[:, kc, 0:n], kc == 0, kc == 15, ["ones_bf", "sq"], [PS(b)])
                rstd_from(b, n, 2048.0, rs[:, 0:n], "rs")
                for kc in range(16):
                    stt("dve", nTc[:, kc, 0:n], xg[:, kc, 0:n], gains_s[:, kc:kc + 1], rs[:, 0:n], ALU.mult, ALU.mult,
                        ["xg", "gains", "rs"], [("nT", sl)])

            def fm_chunk(gi, col0, ncol_out, pbank=None, pcol=0):
                t0, n = groups[gi]
                sl = gi % 2
                b = next_bank() if pbank is None else pbank
                for kc in range(16):
                    mm(ps[b][0:ncol_out, pcol:pcol + n], wkv_s[:, kc, col0:col0 + ncol_out], nT[sl][:, kc, 0:n],
                       kc == 0, kc == 15, [WK(col0), ("nT", sl)], [PS(b)])
                return b

            def body1(gi):
                t0, n = groups[gi]
                sl = gi % 2
                for c in range(2):
                    b = fm_chunk(gi, c * 128, 128)
                    act(ckv_f[:, c, 0:n], ps[b][:, 0:n], AF.Copy, [PS(b)], [("ckv_f", c)])
                    act(ckv_sq[:, c, 0:n], ps[b][:, 0:n], AF.Square, [PS(b)], [("ckv_sq", c)])
                b = fm_chunk(gi, 256, 64)
                fm_chunk(gi, 320, 64, pbank=b, pcol=256)
                tt("dve", r1[:, 0:n], ps[b][0:64, 0:n], ck[sl][:, 0:n], ALU.mult, [PS(b), ("ck", sl)], ["r1"])
                tt("dve", r2[:, 0:n], ps[b][0:64, 256:256 + n], sk[sl][:, 0:n], ALU.mult, [PS(b), ("sk", sl)], ["r2"])
                tt("dve", kr_o[:, 0:n], r1[:, 0:n], r2[:, 0:n], ALU.add, ["r1", "r2"], ["kr_o"])
                dma("sp", kr_s.ap()[:, t0:t0 + n], kr_o[:, 0:n], ["kr_o"], ["kr_s"])
                for m in range(4):
                    b = fm_chunk(gi, 384 + m * 128, 128)
                    evac(kd_o[:, m, 0:n], ps[b][:, 0:n], [PS(b)], ["kd_o"])
                b = next_bank()
                for c in range(2):
                    mm(ps[b][:, 0:n], ones_bf[:, :], ckv_sq[:, c, 0:n], c == 0, c == 1,
                       ["ones_bf", ("ckv_sq", c)], [PS(b)])
                rstd_from(b, n, 256.0, rs2[:, 0:n], "rs2")
                for c in range(2):
                    stt("dve", ckvn[:, c, 0:n], ckv_f[:, c, 0:n], gains_s[:, 68 + c:69 + c], rs2[:, 0:n],
                        ALU.mult, ALU.mult, [("ckv_f", c), "gains", "rs2"], ["ckvn"])

            def body2(gi):
                t0, n = groups[gi]
                sl = gi % 2
                nTc = nT[sl]
                for m in range(4, 8):
                    b = fm_chunk(gi, 384 + m * 128, 128)
                    evac(kd_o[:, m, 0:n], ps[b][:, 0:n], [PS(b)], ["kd_o"])
                dma("sp", kd_s.ap()[:, :, t0:t0 + n].rearrange("m p t -> p m t"), kd_o[:, :, 0:n], ["kd_o"], ["kd_s"])
                tbs = [(0, 128), (128, 128)] if n == 256 else [(0, n)]
                for ti, (o0, tn) in enumerate(tbs):
                    for half in range(2):
                        b = next_bank()
                        cc0 = 1408 + half * 512
                        for kc in range(16):
                            mm(ps[b][0:tn, 0:512], nTc[:, kc, o0:o0 + tn], wkv_s[:, kc, cc0:cc0 + 512],
                               kc == 0, kc == 15, [WK(cc0), ("nT", sl)], [PS(b)])
                        evac(vd_o[0:tn, ti, half * 512:(half + 1) * 512], ps[b][0:tn, 0:512], [PS(b)], [("vd_o", ti)])
                    dma("sp", vd_s.ap()[t0 + o0:t0 + o0 + tn, :], vd_o[0:tn, ti, :], [("vd_o", ti)], ["vd_s"])
                for h in range(8):
                    b = next_bank()
                    for c in range(2):
                        mm(ps[b][:, 0:n], wukv_s[:, c, h * 128:(h + 1) * 128], ckvn[:, c, 0:n], c == 0, c == 1,
                           ["wukv", "ckvn"], [PS(b)])
                    evac(kn_o[:, h, 0:n], ps[b][:, 0:n], [PS(b)], ["kn_o"])
                dma("sp", kn_s.ap()[:, :, t0:t0 + n].rearrange("m p t -> p m t"), kn_o[:, :, 0:n], ["kn_o"], ["kn_s"])
                for ti, (o0, tn) in enumerate(tbs):
                    for half in range(2):
                        b = next_bank()
                        for c in range(2):
                            mm(ps[b][0:tn, 0:512], ckvn[:, c, o0:o0 + tn],
                               wukv_s[:, c, 1024 + half * 512:1024 + (half + 1) * 512], c == 0, c == 1,
                               ["wukv", "ckvn"], [PS(b)])
                        evac(vm_o[0:tn, ti, half * 512:(half + 1) * 512], ps[b][0:tn, 0:512], [PS(b)], [("vm_o", ti)])
                    dma("sp", vm_s.ap()[t0 + o0:t0 + o0 + tn, :], vm_o[0:tn, ti, :], [("vm_o", ti)], ["vm_s"])

            emit_setup()
            load_x(0)
            pre_square(0)
            prologue(0)
            load_x(1)
            for gi in range(len(groups)):
                if 1 <= gi and gi + 1 < len(groups):
                    pre_square(gi + 1)
                body1(gi)
                if gi == 0:
                    pre_square(1)
                if gi + 1 < len(groups):
                    prologue(gi + 1)
                    if gi + 2 < len(groups):
                        load_x(gi + 2)
                if 1 <= gi <= 9:
                    emit_setup_map(gi - 1)
                body2(gi)
            S.barrier()
        setup_ctx.close()


        with ExitStack() as esq:
            qn = sbt(esq, "qn", [128, 8, NQ], BF16, side="right")
            qr = sbt(esq, "qr", [128, 8, NQ], BF16, side="right")
            qd = sbt(esq, "qd", [128, 8, NQ], BF16, side="right")

            with ExitStack() as es:
                wq_s = sbt(es, "wq_s", [128, 16, 1536], BF16)
                wuq_s = sbt(es, "wuq_s", [128, 4, 2048], BF16)
                xg = sbt(es, "xgq", [128, 16, 260], F32)
                sq = sbt(es, "sqq", [128, 16, 260], BF16)
                nTq2 = [sbt(es, "nTq%d" % i, [128, 16, 260], BF16) for i in range(2)]
                rs = sbt(es, "rsq", [128, 260], F32)
                rs2 = sbt(es, "rs2q", [128, 260], F32)
                cq_f = sbt(es, "cq_f", [128, 4, 260], F32)
                cq_sq = sbt(es, "cq_sq", [128, 4, 260], BF16)
                cqn = sbt(es, "cqn", [128, 4, 260], BF16)
                cq_s = [sbt(es, "cosq_s%d" % i, [64, 260], F32) for i in range(2)]
                sq_s = [sbt(es, "sinq_s%d" % i, [64, 260], F32) for i in range(2)]
                r1 = sbt(es, "r1q", [64, 260], F32)
                r2 = sbt(es, "r2q", [64, 260], F32)
                for q, (a_, b_) in enumerate([(0, 512), (512, 1024), (1024, 1536)]):
                    dma("pool", wq_s[:, :, a_:b_], wq.ap()[:, :, a_:b_], ([("wq", q - 1)] if q else ()), [("wq", q)])
                dma("pool", wuq_s[:], wuq.ap(), [("wq", 2)], ["wuq"])
                memset("pool", qr[64:128, :, :], 0.0, ["qr_pad"])
                WQ = []
                QG = [(0, 260), (260, 260), (520, 260), (780, 260)]

                def x_load(gi):
                    c0, n = QG[gi]
                    dma("sp", xg[:, :, 0:n], xq.ap()[:, :, c0:c0 + n], (), ["xg"])

                def cs_load(gi):
                    c0, n = QG[gi]
                    dma("sp", cq_s[gi % 2][:, 0:n], cosq.ap()[:, c0:c0 + n], (), [("cosq", gi % 2)])
                    dma("sp", sq_s[gi % 2][:, 0:n], sinq.ap()[:, c0:c0 + n], (), [("sinq", gi % 2)])

                def q_sq(gi):
                    c0, n = QG[gi]
                    act(sq[:, :, 0:n], xg[:, :, 0:n], AF.Square, ["xg"], ["sq"])

                def q_pro(gi):
                    c0, n = QG[gi]
                    b = next_bank()
                    for kc in range(16):
                        mm(ps[b][:, 0:n], ones_bf[:, :], sq[:, kc, 0:n], kc == 0, kc == 15, ["ones_bf", "sq"], [PS(b)])
                    rstd_from(b, n, 2048.0, rs[:, 0:n], "rs")
                    for kc in range(16):
                        stt("dve", nTq2[gi % 2][:, kc, 0:n], xg[:, kc, 0:n], gains_s[:, kc:kc + 1], rs[:, 0:n],
                            ALU.mult, ALU.mult, ["xg", "gains", "rs"], [("nTq", gi % 2)])

                def q_body_a(gi):
                    c0, n = QG[gi]
                    for c in range(4):
                        b = next_bank()
                        for kc in range(16):
                            mm(ps[b][:, 0:n], wq_s[:, kc, c * 128:(c + 1) * 128], nTq2[gi % 2][:, kc, 0:n],
                               kc == 0, kc == 15, [("wq", 0), ("nTq", gi % 2)], [PS(b)])
                        act(cq_f[:, c, 0:n], ps[b][:, 0:n], AF.Copy, [PS(b)], [("cq_f", c)])
                        act(cq_sq[:, c, 0:n], ps[b][:, 0:n], AF.Square, [PS(b)], [("cq_sq", c)])
                    b = next_bank()
                    for c in range(4):
                        mm(ps[b][:, 0:n], ones_bf[:, :], cq_sq[:, c, 0:n], c == 0, c == 3,
                           ["ones_bf", ("cq_sq", c)], [PS(b)])
                    rstd_from(b, n, 512.0, rs2[:, 0:n], "rs2")
                    for c in range(4):
                        stt("dve", cqn[:, c, 0:n], cq_f[:, c, 0:n], gains_s[:, 64 + c:65 + c], rs2[:, 0:n],
                            ALU.mult, ALU.mult, [("cq_f", c), "gains", "rs2"], ["cqn"])
                    for m in range(8):
                        b = next_bank()
                        for kc in range(16):
                            mm(ps[b][:, 0:n], wq_s[:, kc, 512 + m * 128:512 + (m + 1) * 128], nTq2[gi % 2][:, kc, 0:n],
                               kc == 0, kc == 15, [("wq", 1 + m // 4), ("nTq", gi % 2)], [PS(b)])
                        evac(qd[:, m, c0:c0 + n], ps[b][:, 0:n], [PS(b)], ["qd"])

                def q_body_b(gi):
                    c0, n = QG[gi]
                    for h in range(8):
                        b = next_bank()
                        for c in range(4):
                            mm(ps[b][:, 0:n], wuq_s[:, c, h * 256:h * 256 + 128], cqn[:, c, 0:n], c == 0, c == 3,
                               ["wuq", "cqn"], [PS(b)])
                        evac(qn[:, h, c0:c0 + n], ps[b][:, 0:n], [PS(b)], ["qn"])
                    for h in range(8):
                        ba, bb = next_bank(), next_bank()
                        for c in range(4):
                            mm(ps[ba][0:64, 0:n], wuq_s[:, c, h * 256 + 128:h * 256 + 192], cqn[:, c, 0:n],
                               c == 0, c == 3, ["wuq", "cqn"], [PS(ba)])
                        for c in range(4):
                            mm(ps[bb][0:64, 0:n], wuq_s[:, c, h * 256 + 192:h * 256 + 256], cqn[:, c, 0:n],
                               c == 0, c == 3, ["wuq", "cqn"], [PS(bb)])
                        tt("dve", r1[:, 0:n], ps[ba][0:64, 0:n], cq_s[gi % 2][:, 0:n], ALU.mult,
                           [PS(ba), ("cosq", gi % 2)], ["r1"])
                        tt("dve", r2[:, 0:n], ps[bb][0:64, 0:n], sq_s[gi % 2][:, 0:n], ALU.mult,
                           [PS(bb), ("sinq", gi % 2)], ["r2"])
                        tt("dve", qr[0:64, h, c0:c0 + n], r1[:, 0:n], r2[:, 0:n], ALU.add, ["r1", "r2"], ["qr"])

                x_load(0)
                cs_load(0)
                q_sq(0)
                q_pro(0)
                x_load(1)
                cs_load(1)
                for gi in range(4):
                    if gi + 1 < 4:
                        q_sq(gi + 1)
                        q_pro(gi + 1)
                        if gi + 2 < 4:
                            x_load(gi + 2)
                    q_body_a(gi)
                    q_body_b(gi)
                    if gi + 2 < 4:
                        cs_load(gi + 2)
                S.barrier()

            oT_ctx = ExitStack()
            oT = sbt(oT_ctx, "oT", [128, 16, NQ], BF16)

            with ExitStack() as es:
                kbuf = [sbt(es, "kbuf%d" % i, [128, 2, TKV], BF16) for i in range(2)]
                vbuf = [sbt(es, "vbuf%d" % i, [128, 33, 256], BF16) for i in range(2)]
                krT = sbt(es, "krT", [128, TKV], BF16)
                pT = [sbt(es, "pT%d" % i, [128, 390], BF16) for i in range(5)]
                rec = sbt(es, "rec", [128, 390], F32)
                A_s = sbt(es, "A_s", [128, 2, 390], F32)
                tmpd = sbt(es, "tmpd", [128, 390], F32)
                Oc = sbt(es, "Oc", [128, 2, 390], F32)
                sqd = sbt(es, "sqd", [128, 2, 390], BF16)
                rsd = sbt(es, "rsd", [128, 390], F32)
                dma("sp", krT[0:64, :], kr_s.ap(), ["kr_s"], ["krT"])
                memset("pool", krT[64:128, :], 0.0, ["krT_pad"])
                mla_scale = 1.0 / math.sqrt(192.0)
                diff_scale = 1.0 / math.sqrt(128.0)

                def load_unit(u):
                    sl = u % 2
                    if u < 8:
                        dma("sp", kbuf[sl][:, 0, :], kn_s.ap()[u], ["kn_s"], [("kbuf", sl)])
                        dma("sp", vbuf[sl][:, 0:32, 0:128],
                            vm_s.ap()[0:4096, u * 128:(u + 1) * 128].rearrange("(b p) f -> p b f", p=128),
                            ["vm_s"], [("vbuf", sl)])
                        dma("sp", vbuf[sl][0:16, 32, 0:128], vm_s.ap()[4096:4112, u * 128:(u + 1) * 128],
                            ["vm_s"], [("vbuf", sl)])
                    else:
                        hd = u - 8
                        dma("sp", kbuf[sl][:, :, :], kd_s.ap()[2 * hd:2 * hd + 2].rearrange("m p t -> p m t"),
                            ["kd_s"], [("kbuf", sl)])
                        dma("sp", vbuf[sl][:, 0:32, :],
                            vd_s.ap()[0:4096, hd * 256:(hd + 1) * 256].rearrange("(b p) f -> p b f", p=128),
                            ["vd_s"], [("vbuf", sl)])
                        dma("sp", vbuf[sl][0:16, 32, :], vd_s.ap()[4096:4112, hd * 256:(hd + 1) * 256],
                            ["vd_s"], [("vbuf", sl)])

                srot = {"n": 0}
                orot = {"n": 0}
                prot = {"n": 0}
                SKEW = 2

                jobs = []
                for u in range(12):
                    is_mla = u < 8
                    hd = u - 8
                    for (r0, r1e) in GB:
                        for cmap in ([0] if is_mla else [0, 1]):
                            oset = (4 + 2 * (orot["n"] % 2)) if is_mla else 5
                            orot["n"] += 1
                            jobs.append(dict(u=u, sl=u % 2, is_mla=is_mla, hd=hd, r0=r0, r1e=r1e, c0=QB * r0,
                                             width=QB * (r1e - r0), cmap=cmap,
                                             mi=(8 if is_mla else 2 * hd + cmap),
                                             bO=[oset, oset + 1][:(1 if is_mla else 2)], bS=(oset + 1 if is_mla else oset + 2),
                                             nsb=(4 if is_mla else 5), skew=(3 if is_mla else 4),
                                             kblocks=["meta"] + list(range(0, 4 * (r1e - 1) + 4))))

                def geom(J, kb):
                    if kb == "meta":
                        kk, k0, vblk, ra = 16, 4096, 32, J["r0"]
                    else:
                        kk, k0, vblk, ra = 128, 128 * kb, kb, max(J["r0"], kb // 4)
                    a0 = QB * (ra - J["r0"])
                    return kk, k0, vblk, ra, a0, J["width"] - a0, J["c0"] + a0

                def stage_A(J, ki):
                    kb = J["kblocks"][ki]
                    kk, k0, vblk, ra, a0, N, q0 = geom(J, kb)
                    u, sl, hd, cmap, mi = J["u"], J["sl"], J["hd"], J["cmap"], J["mi"]
                    sb_ = srot["n"] % J["nsb"]
                    srot["n"] += 1
                    pS = ps[sb_]
                    if J["is_mla"]:
                        mm(pS[0:kk, 0:N], kbuf[sl][:, 0, k0:k0 + kk], qn[:, u, q0:q0 + N], True, False,
                           [("kbuf", sl), "qn"], [PS(sb_)])
                        mm(pS[0:kk, 0:N], krT[:, k0:k0 + kk], qr[:, u, q0:q0 + N], False, True,
                           ["krT", "krT_pad", "qr", "qr_pad"], [PS(sb_)])
                    else:
                        mm(pS[0:kk, 0:N], kbuf[sl][:, cmap, k0:k0 + kk], qd[:, 2 * hd + cmap, q0:q0 + N],
                           True, True, [("kbuf", sl), "qd"], [PS(sb_)])
                    for r in range(ra, J["r1e"]):
                        cc = QB * (r - ra)
                        if kb == "meta":
                            if r == 0:
                                tt("dve", pS[0:16, cc:cc + QB], pS[0:16, cc:cc + QB],
                                   btile[0:16, mi * 6 + 5, :], ALU.add, [PS(sb_), "btile"], [PS(sb_)])
                        else:
                            jj = kb - 4 * r
                            if -1 <= jj <= 3:
                                tt("dve", pS[:, cc:cc + QB], pS[:, cc:cc + QB],
                                   btile[:, mi * 6 + jj + 1, :], ALU.add, [PS(sb_), "btile"], [PS(sb_)])
                    pi = prot["n"] % 5
                    prot["n"] += 1
                    if J["is_mla"]:
                        act(pT[pi][0:kk, 0:N], pS[0:kk, 0:N], AF.Exp, [PS(sb_)], [("pT", pi)],
                            bias=0.0, scale=mla_scale)
                    else:
                        act(pT[pi][0:kk, 0:N], pS[0:kk, 0:N], AF.Exp, [PS(sb_), "cm"], [("pT", pi)],
                            bias=cm_s[0:kk, mi:mi + 1], scale=diff_scale)
                    return pi

                def stage_B(J, ki, pi):
                    kb = J["kblocks"][ki]
                    kk, k0, vblk, ra, a0, N, q0 = geom(J, kb)
                    sl = J["sl"]
                    first = ki == 0
                    last = ki == len(J["kblocks"]) - 1
                    for d, bo in enumerate(J["bO"]):
                        mm(ps[bo][:, a0:a0 + N], vbuf[sl][0:kk, vblk, d * 128:(d + 1) * 128],
                           pT[pi][0:kk, 0:N], first, last, [("vbuf", sl), ("pT", pi)], [PS(bo)])
                    mm(ps[J["bS"]][:, a0:a0 + N], ones_bf[0:kk, :], pT[pi][0:kk, 0:N], first, last,
                       ["ones_bf", ("pT", pi)], [PS(J["bS"])])

                later = []

                def stage_F(J):
                    u, hd, cmap, c0, width, bO, bS = J["u"], J["hd"], J["cmap"], J["c0"], J["width"], J["bO"], J["bS"]
                    if not J["is_mla"]:
                        for d in range(2):
                            vcopy(Oc[:, d, 0:width], ps[bO[d]][:, 0:width], [PS(bO[d])], [("Oc", d)])
                    act(rec[:, 0:width], ps[bS][:, 0:width], AF.Ln, [PS(bS)], ["rec"])
                    act(rec[:, 0:width], rec[:, 0:width], AF.Exp, ["rec"], ["rec"], bias=0.0, scale=-1.0)
                    if J["is_mla"]:
                        tt("dve", oT[:, u, c0:c0 + width], ps[bO[0]][:, 0:width], rec[:, 0:width], ALU.mult,
                           [PS(bO[0]), "rec"], ["oT"])
                    elif cmap == 0:
                        for d in range(2):
                            tt("dve", A_s[:, d, 0:width], Oc[:, d, 0:width], rec[:, 0:width], ALU.mult,
                               [("Oc", d), "rec"], [("A", d)])
                    else:
                        for d in range(2):
                            tt("dve", tmpd[:, 0:width], Oc[:, d, 0:width], rec[:, 0:width], ALU.mult,
                               [("Oc", d), "rec"], ["tmpd"])
                            stt("dve", A_s[:, d, 0:width], tmpd[:, 0:width], lam_s[:, 1:2], A_s[:, d, 0:width],
                                ALU.mult, ALU.add, ["tmpd", "lam", ("A", d)], [("A", d)])
                            tt("dve", sqd[:, d, 0:width], A_s[:, d, 0:width], A_s[:, d, 0:width], ALU.mult,
                               [("A", d)], [("sqd", d)])

                        def _f2(hd=hd, c0=c0, width=width):
                            sb_ = srot["n"] % 5
                            srot["n"] += 1
                            for d in range(2):
                                mm(ps[sb_][:, 0:width], ones_bf[:, :], sqd[:, d, 0:width], d == 0, d == 1,
                                   ["ones_bf", ("sqd", d)], [PS(sb_)])
                            rstd_from(sb_, width, 256.0, rsd[:, 0:width], "rsd")
                            for d in range(2):
                                stt("dve", oT[:, 8 + 2 * hd + d, c0:c0 + width], A_s[:, d, 0:width],
                                    gsub_s[:, d:d + 1], rsd[:, 0:width], ALU.mult, ALU.mult,
                                    [("A", d), "gsub", "rsd"], ["oT"])
                        later.append([3, _f2])

                def tick():
                    for it in list(later):
                        it[0] -= 1
                        if it[0] <= 0:
                            later.remove(it)
                            it[1]()

                load_unit(0)
                load_unit(1)
                b_started = set()
                pend = []

                def do_B():
                    Jb, kib, pib = pend.pop(0)
                    ub = Jb["u"]
                    if ub not in b_started:
                        b_started.add(ub)
                        if ub >= 1 and ub + 1 < 12:
                            load_unit(ub + 1)
                    stage_B(Jb, kib, pib)
                    if kib == len(Jb["kblocks"]) - 1:
                        stage_F(Jb)

                for J in jobs:
                    for ki in range(len(J["kblocks"])):
                        pi = stage_A(J, ki)
                        pend.append((J, ki, pi))
                        while len(pend) > J["skew"]:
                            do_B()
                        tick()
                while pend:
                    do_B()
                    tick()
                for _ in range(4):
                    tick()
                S.barrier()
        right_ctx.close()
        n2_ctx = ExitStack()
        n2T = sbt(n2_ctx, "n2T", [128, 16, NQ], BF16, side="right")
        wup_b = [sbt(n2_ctx, "wup_b%d" % i, [128, 16, 256], BF16, side="right") for i in range(3)]

        def load_wup(f):
            dma("pool", wup_b[f % 3][:, :, :], wup.ap()[f].rearrange("p (k n) -> p k n", n=256), (),
                [("wup_b", f % 3)])
        with ExitStack() as es:
            aT = sbt(es, "aT", [128, 16, NQ], F32)
            wo_b = [sbt(es, "wo_b%d" % i, [128, 2, 2048], BF16) for i in range(2)]
            sqb = [sbt(es, "sqb%d" % i, [128, 390], BF16) for i in range(3)]
            rs1 = sbt(es, "rs1", [128, NQ], F32)
            rs2 = rs1
            xqc = [sbt(es, "xqc%d" % i, [128, NQ], F32) for i in range(4)]

            def load_xq(ch):
                dma("sp", xqc[ch % 4][:], xq.ap()[:, ch, :], (), [("xqc", ch % 4)])

            def load_wo(g):
                dma("pool", wo_b[g % 2][:, :, :], wo.ap()[2 * g:2 * g + 2].rearrange("o p f -> p o f"), (),
                    [("wo_b", g % 2)])

            load_wo(0)
            load_wo(1)
            for ch in range(4):
                load_xq(ch)
            pending = []
            sqi = 0
            for oc in range(16):
                g = oc // 2
                if oc % 2 == 0 and 2 <= g + 1 < 8:
                    load_wo(g + 1)
                for gi, (c0, n) in enumerate(CG):
                    b = next_bank(0, 5)
                    for kc in range(16):
                        mm(ps[b][:, 0:n], wo_b[g % 2][:, oc % 2, kc * 128:(kc + 1) * 128], oT[:, kc, c0:c0 + n],
                           kc == 0, kc == 15, [("wo_b", g % 2), "oT"], [PS(b)])
                    for f in pending:
                        f()
                    pending = []
                    si = sqi % 3
                    sqi += 1
                    act(aT[:, oc, c0:c0 + n], ps[b][:, 0:n], AF.Copy, [PS(b)], [("aT", oc)])
                    act(sqb[si][:, 0:n], ps[b][:, 0:n], AF.Square, [PS(b)], [("sqb", si)])

                    def _f(si=si, gi=gi, n=n, oc=oc):
                        mm(ps[5 + gi][:, 0:n], ones_bf[:, :], sqb[si][:, 0:n], oc == 0, oc == 15,
                           ["ones_bf", ("sqb", si)], [PS(5 + gi)])
                    pending.append(_f)
            for f in pending:
                f()
            pending = []
            load_wup(0)
            load_wup(1)
            for gi, (c0, n) in enumerate(CG):
                rstd_from(5 + gi, n, 2048.0, rs1[:, c0:c0 + n], ("rs1", gi))
            RS1 = [("rs1", gi) for gi in range(3)]
            sqi = 0
            for ch in range(16):
                xs = ch % 4
                stt("dve", aT[:, ch, :], aT[:, ch, :], gains_s[:, 16 + ch:17 + ch], rs1[:, :], ALU.mult, ALU.mult,
                    [("aT", ch), "gains"] + RS1, [("aT", ch)])
                tt("dve", aT[:, ch, :], aT[:, ch, :], xqc[xs][:], ALU.add,
                   [("aT", ch), ("xqc", xs)], [("aT", ch)])
                if ch + 4 < 16:
                    load_xq(ch + 4)
                for gi, (c0, n) in enumerate(CG):
                    si = sqi % 3
                    sqi += 1
                    act(sqb[si][:, 0:n], aT[:, ch, c0:c0 + n], AF.Square, [("aT", ch)], [("sqb", si)])
                    mm(ps[5 + gi][:, 0:n], ones_bf[:, :], sqb[si][:, 0:n], ch == 0, ch == 15,
                       ["ones_bf", ("sqb", si)], [PS(5 + gi)])
                dma("act", h1_s.ap()[:, ch, :].rearrange("p (r t) -> p r t", t=128),
                    aT[:, ch, :].rearrange("p (r c) -> p r c", c=QB)[:, :, 2:QB], [("aT", ch)], ["h1_s"])
            for gi, (c0, n) in enumerate(CG):
                rstd_from(5 + gi, n, 2048.0, rs2[:, c0:c0 + n], ("rs1", gi))
            RS2 = [("rs1", gi) for gi in range(3)]
            for ch in range(16):
                stt("dve", n2T[:, ch, :], aT[:, ch, :], gains_s[:, 32 + ch:33 + ch],
                    rs2[:, :], ALU.mult, ALU.mult, [("aT", ch), "gains"] + RS2, [("n2T", ch)])
            S.barrier()
        oT_ctx.close()

        act_ctx = ExitStack()
        actT = sbt(act_ctx, "actT", [128, 44, 1024], BF16)
        wdn_b = [sbt(act_ctx, "wdn_b%d" % i, [128, 44, 128], BF16) for i in range(2)]

        def load_wdn(oc):
            dma("pool", wdn_b[oc % 2][:, :, :], wdn.ap()[oc].rearrange("p (k n) -> p k n", n=128), (),
                [("wdn_b", oc % 2)])
        with ExitStack() as es:
            tg = [sbt(es, "tg%d" % i, [128, 3, 128], F32) for i in range(2)]
            tv = [sbt(es, "tv%d" % i, [128, 3, 128], F32) for i in range(2)]
            sg = [sbt(es, "sg%d" % i, [128, 3, 128], F32) for i in range(2)]

            ti = 0
            for f in range(44):
                if f + 2 < 44:
                    load_wup(f + 2)
                if f == 36:
                    load_wdn(0)
                    load_wdn(1)
                for gi, (c0, n) in enumerate(CG):
                    nb = n // QB
                    r0 = GB[gi][0]
                    sl = ti % 2
                    ti += 1
                    tgs = [tg[sl], tv[sl]]
                    for part in range(2):
                        b = next_bank()
                        for kc in range(16):
                            mm(ps[b][:, 0:n], wup_b[f % 3][:, kc, part * 128:(part + 1) * 128], n2T[:, kc, c0:c0 + n],
                               kc == 0, kc == 15, [("wup_b", f % 3), "n2T"], [PS(b)])
                        pv = ps[b][:, 0:n].rearrange("p (r c) -> p r c", c=QB)
                        cb = f * 8 + part * 4
                        tk = ("t", part, sl)
                        act(tgs[part][:, 0:nb, :], pv[:, :, 2:QB], AF.Identity, [PS(b), "convp"], [tk],
                            bias=convp_s[:, cb + 3:cb + 4], scale=convp_s[:, cb + 2:cb + 3])
                        stt("dve", tgs[part][:, 0:nb, :], pv[:, :, 1:QB - 1], convp_s[:, cb + 1:cb + 2],
                            tgs[part][:, 0:nb, :], ALU.mult, ALU.add, [PS(b), "convp", tk], [tk])
                        stt("dve", tgs[part][:, 0:nb, :], pv[:, :, 0:QB - 2], convp_s[:, cb:cb + 1],
                            tgs[part][:, 0:nb, :], ALU.mult, ALU.add, [PS(b), "convp", tk], [tk])
                    act(sg[sl][:, 0:nb, :], tg[sl][:, 0:nb, :], AF.Silu, [("t", 0, sl)], [("sg", sl)])
                    tt("pool", actT[:, f, r0 * 128:(r0 + nb) * 128].rearrange("p (r t) -> p r t", t=128),
                       sg[sl][:, 0:nb, :], tv[sl][:, 0:nb, :], ALU.mult, [("sg", sl), ("t", 1, sl)], [("actT", f)])
            S.barrier()
        n2_ctx.close()

        with ExitStack() as es:
            fT = sbt(es, "fT", [128, 16, 1024], F32)
            sqb = [sbt(es, "sqc%d" % i, [128, 512], BF16) for i in range(3)]
            rs3 = sbt(es, "rs3", [128, 1024], F32)
            h1c = [sbt(es, "h1c%d" % i, [128, 1024], F32) for i in range(3)]
            ACTK = [("actT", f) for f in range(44)]

            pending = []
            sqi = 0
            for oc in range(16):
                if 2 <= oc + 1 < 16:
                    load_wdn(oc + 1)
                for tgi in range(2):
                    b = next_bank(0, 6)
                    for kc in range(44):
                        mm(ps[b][:, 0:512], wdn_b[oc % 2][:, kc, :], actT[:, kc, tgi * 512:(tgi + 1) * 512],
                           kc == 0, kc == 43, [("wdn_b", oc % 2)] + (ACTK if kc == 0 else []), [PS(b)])
                    for f_ in pending:
                        f_()
                    pending = []
                    si = sqi % 3
                    sqi += 1
                    act(fT[:, oc, tgi * 512:(tgi + 1) * 512], ps[b][:, 0:512], AF.Copy, [PS(b)], [("fT", oc)])
                    act(sqb[si][:, :], ps[b][:, 0:512], AF.Square, [PS(b)], [("sqb", si)])

                    def _f(si=si, tgi=tgi, oc=oc):
                        mm(ps[6 + tgi][:, 0:512], ones_bf[:, :], sqb[si][:, :], oc == 0, oc == 15,
                           ["ones_bf", ("sqb", si)], [PS(6 + tgi)])
                    pending.append(_f)
            for f_ in pending:
                f_()
            for tgi in range(2):
                rstd_from(6 + tgi, 512, 2048.0, rs3[:, tgi * 512:(tgi + 1) * 512], ("rs3", tgi))
            RS3 = [("rs3", 0), ("rs3", 1)]
            outs = []
            for ch in range(16):
                xs = ch % 3
                dma("sp", h1c[xs][:], h1_s.ap()[:, ch, :], ["h1_s"], [("h1c", xs)])
                stt("dve", fT[:, ch, :], fT[:, ch, :], gains_s[:, 48 + ch:49 + ch], rs3[:, :], ALU.mult, ALU.mult,
                    [("fT", ch), "gains"] + RS3, [("fT", ch)])
                tt("dve", fT[:, ch, :], fT[:, ch, :], h1c[xs][:], ALU.add,
                   [("fT", ch), ("h1c", xs)], [("fT", ch)])
                dma("act", out.ap()[:, ch, :], fT[:, ch, :], [("fT", ch)], [("out", ch)])
                outs.append(("out", ch))
            S.add("sp", None, reads=outs)
        act_ctx.close()

        S.finalize()
        sems = {s: top.enter_context(nc.semaphore("s_" + s)) for s in S.streams}
        block = top.enter_context(nc.Block())
        block.tensor(lambda e: S.emit("pe", e, sems))
        block.scalar(lambda e: S.emit("act", e, sems))
        block.vector(lambda e: S.emit("dve", e, sems))
        block.gpsimd(lambda e: S.emit("pool", e, sems))
        block.sync(lambda e: S.emit("sp", e, sems))
    return nc


def _t5_bucket(n):
    n = np.maximum(n, 0)
    nf = np.maximum(n, 1).astype(np.float32)
    large = 16 + (np.log(nf / np.float32(16.0)) / np.float32(math.log(8.0)) * np.float32(16.0)).astype(np.int32)
    large = np.minimum(large, 31)
    return np.where(n < 16, n, large)


def _fm(w, ncols=None):
    K, N = w.shape
    return np.ascontiguousarray(w.reshape(K // 128, 128, N).transpose(1, 0, 2))


def _prep_shared(inp):
    f = np.float32
    w_in = inp["w_in"][0]
    cq, ckv, kr, qdw, kdw, vdw = np.split(w_in, np.cumsum([512, 256, 64, 1024, 1024])[:], axis=1)
    kr_sw = np.concatenate([kr[:, 32:64], kr[:, 0:32]], axis=1)
    sh = {}
    sh["wkv"] = _fm(np.concatenate([ckv, kr, kr_sw, kdw, vdw], axis=1))
    sh["wq"] = _fm(np.concatenate([cq, qdw], axis=1))
    w_ukv = inp["w_ukv"][0].reshape(256, 8, 256)
    sh["wukv"] = _fm(np.concatenate([w_ukv[:, :, 0:128].reshape(256, 1024), w_ukv[:, :, 128:256].reshape(256, 1024)], axis=1))
    w_uq = inp["w_uq"][0].reshape(512, 8, 192)
    wuq = np.concatenate([w_uq[:, :, 0:128], w_uq[:, :, 128:192], w_uq[:, :, 160:192], w_uq[:, :, 128:160]], axis=2)
    sh["wuq"] = _fm(wuq.reshape(512, 2048))
    w_o = inp["w_o"][0]
    sh["wo"] = np.ascontiguousarray(w_o.reshape(16, 128, 16, 128).transpose(2, 1, 0, 3).reshape(16, 128, 2048))
    w_up = inp["w_up"][0]
    g = w_up[:, :5632].reshape(16, 128, 44, 128)
    v = w_up[:, 5632:].reshape(16, 128, 44, 128)
    gv = np.concatenate([g, v], axis=3)
    sh["wup"] = np.ascontiguousarray(gv.transpose(2, 1, 0, 3).reshape(44, 128, 4096))
    w_dn = inp["w_down"][0]
    sh["wdn"] = np.ascontiguousarray(w_dn.reshape(44, 128, 16, 128).transpose(2, 1, 0, 3).reshape(16, 128, 5632))
    cols = []
    for k in ("g_attn_pre", "g_attn_post", "g_ffn_pre", "g_ffn_post"):
        cols.append(inp[k][0].reshape(16, 128).T)
    cols.append(inp["g_cq"][0].reshape(4, 128).T)
    cols.append(inp["g_ckv"][0].reshape(2, 128).T)
    cols.append(inp["g_diff_sub"][0].reshape(2, 128).T)
    sh["gains"] = np.ascontiguousarray(np.concatenate(cols, axis=1).astype(f))
    cw = inp["conv_w"][0]
    cb = inp["conv_b"][0]
    cp = np.zeros((128, 44, 2, 4), f)
    for part in range(2):
        o = part * 5632
        for j in range(3):
            cp[:, :, part, j] = cw[j, o:o + 5632].reshape(44, 128).T
        cp[:, :, part, 3] = cb[o:o + 5632].reshape(44, 128).T
    sh["convp"] = np.ascontiguousarray(cp.reshape(128, 352))
    sh["rb"] = np.ascontiguousarray(inp["rel_bias"].astype(f))
    sh["rb31"] = np.ascontiguousarray(np.repeat(inp["rel_bias"][31:32, :], 128, axis=0).astype(f))
    sh["lamv"] = np.ascontiguousarray(np.stack([inp["lambda_q1"][0], inp["lambda_q2"][0],
                                                inp["lambda_k1"][0], inp["lambda_k2"][0]], axis=1).astype(f))
    inv = 10000.0 ** (-np.arange(32, dtype=np.float64) / 32.0)
    posk = np.concatenate([16 + np.arange(4096), np.arange(16)]).astype(np.float64)
    ang = posk[None, :] * inv[:, None]
    sh["cosk"] = np.ascontiguousarray(np.concatenate([np.cos(ang), np.cos(ang)], axis=0).astype(f))
    sh["sink"] = np.ascontiguousarray(np.concatenate([-np.sin(ang), np.sin(ang)], axis=0).astype(f))
    sh["_inv"] = inv
    return sh


def _prep_core(inp, sh, c, xkv_b):
    f = np.float32
    b, i = c // 4, c % 4
    m = {k: v for k, v in sh.items() if not k.startswith("_")}
    m["xkv"] = xkv_b[b]
    seq = np.concatenate([inp["meta_tokens"].astype(f), inp["x"][b]], axis=0)
    pos = np.concatenate([16 + 128 * (4 * r + i) - 2 + np.arange(QB) for r in range(8)])
    xo = seq[pos]
    m["xq"] = np.ascontiguousarray(xo.T.reshape(16, 128, NQ).transpose(1, 0, 2))
    ang = pos.astype(np.float64)[None, :] * sh["_inv"][:, None]
    m["cosq"] = np.ascontiguousarray(np.concatenate([np.cos(ang), np.cos(ang)], axis=0).astype(f))
    m["sinq"] = np.ascontiguousarray(np.concatenate([-np.sin(ang), np.sin(ang)], axis=0).astype(f))
    u = np.arange(769) - 513
    rrel = u + 128 * i
    ohm = np.zeros((33, 769), f)
    bk = _t5_bucket(rrel)
    for idx in range(769):
        if rrel[idx] < 0:
            ohm[32, idx] = 1.0
        else:
            ohm[bk[idx], idx] = 1.0
    m["oh"] = ohm
    return m


_NC_CACHE = {}


def kernel(**inputs):
    inp = {k: np.asarray(v) for k, v in inputs.items()}
    if "nc" not in _NC_CACHE:
        _NC_CACHE["nc"] = build_nc()
    nc = _NC_CACHE["nc"]
    sh = _prep_shared(inp)
    xkv_b = []
    for b in range(2):
        seqk = np.concatenate([inp["x"][b], inp["meta_tokens"].astype(np.float32)], axis=0)
        xkv_b.append(np.ascontiguousarray(seqk.T.reshape(16, 128, TKV).transpose(1, 0, 2)))
    in_maps = [_prep_core(inp, sh, c, xkv_b) for c in range(NCORES)]
    res = run_bass_kernel_spmd(nc, in_maps, core_ids=list(range(NCORES)))
    outp = np.zeros((2, 4096, 2048), np.float32)
    for c in range(NCORES):
        b, i = c // 4, c % 4
        o = np.asarray(res.results[c]["out"]).reshape(128, 16, 8, 128)
        for r in range(8):
            g = 4 * r + i
            outp[b, 128 * g:128 * (g + 1), :] = o[:, :, r, :].transpose(2, 1, 0).reshape(128, 2048)
    return outp
```
